# Optimizing a Trainium2 kernel written in Bass

```python
import jax, jax.numpy as jnp
from jax import lax
import numpy as np

D_MODEL = 1024
BATCH = 2
SEQ = 8192
DEPTH = 2

GRID_W = 64
EPS = 1e-6

A_HEADS = 8
A_HEAD_DIM = D_MODEL // 16
A_WIDTH = A_HEADS * A_HEAD_DIM
CONV_WIDTH = 3
B_GROUPS = 4
B_GROUP_DIM = D_MODEL // 8
B_WIDTH = B_GROUPS * B_GROUP_DIM
EVEN_IN = 4 * A_WIDTH + 2 * B_WIDTH
EVEN_MIX = A_WIDTH + B_WIDTH

HEAD_DIM = 128
N_HEADS = D_MODEL // HEAD_DIM
N_KV_HEADS = N_HEADS // 4
Q_WIDTH = N_HEADS * HEAD_DIM
KV_WIDTH = N_KV_HEADS * HEAD_DIM
ODD_IN = 2 * Q_WIDTH + 2 * KV_WIDTH
ROPE_THETA = 10000.0
Q_BLOCK = 128

N_EVEN = (DEPTH + 1) // 2
N_ODD = DEPTH // 2

kernel_name = "hybrid_shortconv_fourier_axial_gqa_encoder"


def rms_norm(x, g):
    xf = x.astype(jnp.float32)
    y = xf * lax.rsqrt(jnp.mean(xf * xf, axis=-1, keepdims=True) + EPS)
    return (y * g.astype(jnp.float32)).astype(x.dtype)


def centred_depthwise_conv(u, w):
    s = u.shape[1]
    pad = CONV_WIDTH // 2
    up = jnp.pad(u, ((0, 0), (pad, CONV_WIDTH - 1 - pad), (0, 0)))
    out = up[:, 0:s] * w[0]
    for tap in range(1, CONV_WIDTH):
        out = out + up[:, tap:tap + s] * w[tap]
    return out


def short_conv_fourier_mixer(h, w_in, conv_w, w_out):
    bsz, s, _ = h.shape
    proj = h @ w_in
    a_x, a_b, a_c, a_z, b_u, b_z = jnp.split(
        proj,
        [A_WIDTH, 2 * A_WIDTH, 3 * A_WIDTH, 4 * A_WIDTH, 4 * A_WIDTH + B_WIDTH],
        axis=-1)
    y_a = a_b * centred_depthwise_conv(a_c * a_x, conv_w) * jax.nn.silu(a_z)
    u = b_u.reshape(bsz, s, B_GROUPS, B_GROUP_DIM).astype(jnp.float32)
    f = jnp.fft.fft2(u, axes=(1, 3), norm="ortho").real
    y_b = f.reshape(bsz, s, B_WIDTH).astype(h.dtype) * jax.nn.silu(b_z)
    return jnp.concatenate([y_a, y_b], axis=-1) @ w_out


def axial_rope_tables(s):
    rows = s // GRID_W
    row = jnp.repeat(jnp.arange(rows), GRID_W).astype(jnp.float32)
    col = jnp.tile(jnp.arange(GRID_W), rows).astype(jnp.float32)
    n_pair = HEAD_DIM // 4
    inv = ROPE_THETA ** (-jnp.arange(n_pair, dtype=jnp.float32) / n_pair)
    ang = jnp.concatenate([row[:, None] * inv, col[:, None] * inv], axis=-1)
    return jnp.cos(ang), jnp.sin(ang)


def apply_rope(x, cos, sin):
    xf = x.astype(jnp.float32).reshape(x.shape[:-1] + (HEAD_DIM // 2, 2))
    x1, x2 = xf[..., 0], xf[..., 1]
    c = cos[None, :, None, :]
    sn = sin[None, :, None, :]
    out = jnp.stack([x1 * c - x2 * sn, x1 * sn + x2 * c], axis=-1)
    return out.reshape(x.shape).astype(x.dtype)


def gqa_axial_mixer(h, w_in, q_gain, k_gain, w_out):
    bsz, s, _ = h.shape
    proj = h @ w_in
    q, k, v, z = jnp.split(
        proj, [Q_WIDTH, Q_WIDTH + KV_WIDTH, Q_WIDTH + 2 * KV_WIDTH], axis=-1)
    q = q.reshape(bsz, s, N_HEADS, HEAD_DIM)
    k = k.reshape(bsz, s, N_KV_HEADS, HEAD_DIM)
    v = v.reshape(bsz, s, N_KV_HEADS, HEAD_DIM)
    cos, sin = axial_rope_tables(s)
    q = apply_rope(rms_norm(q, q_gain), cos, sin)
    k = apply_rope(rms_norm(k, k_gain), cos, sin)
    groups = N_HEADS // N_KV_HEADS
    n_blk = s // Q_BLOCK
    qb = q.reshape(bsz, n_blk, Q_BLOCK, N_KV_HEADS, groups, HEAD_DIM).transpose(1, 0, 3, 4, 2, 5)
    kt = k.transpose(0, 2, 1, 3)
    vt = v.transpose(0, 2, 1, 3)
    scale = HEAD_DIM ** -0.5

    def attend_block(q_blk):
        sc = jnp.einsum('bkgqd,bksd->bkgqs', q_blk, kt,
                        preferred_element_type=jnp.float32) * scale
        p = jax.nn.softmax(sc, axis=-1)
        return jnp.einsum('bkgqs,bksd->bkgqd', p.astype(vt.dtype), vt)

    o = lax.map(attend_block, qb)
    o = o.transpose(1, 0, 4, 2, 3, 5).reshape(bsz, s, Q_WIDTH)
    return (o * jax.nn.silu(z)) @ w_out


def setup_inputs(seed: int = 0) -> dict:
    key = jax.random.key(seed)
    ks = jax.random.split(key, 12)
    f32 = jnp.float32
    x = jax.random.normal(ks[0], (BATCH, SEQ, D_MODEL), f32)
    norm_even = 1.0 + 0.02 * jax.random.normal(ks[1], (N_EVEN, D_MODEL), f32)
    w_in_even = jax.random.normal(ks[2], (N_EVEN, D_MODEL, EVEN_IN), f32) * D_MODEL ** -0.5
    conv_w = jax.random.normal(ks[3], (N_EVEN, CONV_WIDTH, A_WIDTH), f32) * CONV_WIDTH ** -0.5
    w_out_even = jax.random.normal(ks[4], (N_EVEN, EVEN_MIX, D_MODEL), f32) * EVEN_MIX ** -0.5
    norm_odd = 1.0 + 0.02 * jax.random.normal(ks[5], (N_ODD, D_MODEL), f32)
    w_in_odd = jax.random.normal(ks[6], (N_ODD, D_MODEL, ODD_IN), f32) * D_MODEL ** -0.5
    q_gain = 1.0 + 0.02 * jax.random.normal(ks[7], (N_ODD, HEAD_DIM), f32)
    k_gain = 1.0 + 0.02 * jax.random.normal(ks[8], (N_ODD, HEAD_DIM), f32)
    w_out_odd = jax.random.normal(ks[9], (N_ODD, Q_WIDTH, D_MODEL), f32) * Q_WIDTH ** -0.5
    final_norm = 1.0 + 0.02 * jax.random.normal(ks[10], (D_MODEL,), f32)
    return {"x": x, "norm_even": norm_even, "w_in_even": w_in_even, "conv_w": conv_w,
            "w_out_even": w_out_even, "norm_odd": norm_odd, "w_in_odd": w_in_odd,
            "q_gain": q_gain, "k_gain": k_gain, "w_out_odd": w_out_odd,
            "final_norm": final_norm}


def reference(x, norm_even, w_in_even, conv_w, w_out_even, norm_odd, w_in_odd,
              q_gain, k_gain, w_out_odd, final_norm):
    for layer in range(DEPTH):
        i = layer // 2
        if layer % 2 == 0:
            x = x + short_conv_fourier_mixer(rms_norm(x, norm_even[i]), w_in_even[i],
                                             conv_w[i], w_out_even[i])
        else:
            x = x + gqa_axial_mixer(rms_norm(x, norm_odd[i]), w_in_odd[i],
                                    q_gain[i], k_gain[i], w_out_odd[i])
    return rms_norm(x, final_norm)
```

```python
import os
import numpy as np
import ml_dtypes
import concourse.bass as bass
import concourse.mybir as mybir
from concourse.bass_utils import run_bass_kernel_spmd
from contextlib import ExitStack

F32 = mybir.dt.float32
BF16 = mybir.dt.bfloat16
AF = mybir.ActivationFunctionType
ALU = mybir.AluOpType
AX = mybir.AxisListType
NPBF = ml_dtypes.bfloat16

NCORES = 8
D = 1024
S = 8192
T = 2048
NT = T // 128
EPS = 1e-6


class Prog:
    ENGS = ("pe", "act", "dve", "pool", "sp")

    def __init__(self, nc, ndma_sems=32):
        self.nc = nc
        self.ops = []
        self.per_eng = {e: [] for e in self.ENGS}
        self.last_w = {}
        self.readers = {}
        self.ndma_sems = ndma_sems

    def op(self, eng, fn, reads=(), writes=(), dma=False, nosignal=False, own_sem=None):
        o = dict(eng=eng, fn=fn, dma=dma, deps=set(), idx=len(self.ops), signal=False, nosignal=nosignal, own_sem=own_sem)
        for r in reads:
            w = self.last_w.get(r)
            if w is not None:
                o["deps"].add(w)
        for r in writes:
            w = self.last_w.get(r)
            if w is not None:
                o["deps"].add(w)
            for rd in self.readers.get(r, ()):
                o["deps"].add(rd)
        o["deps"].discard(o["idx"])
        for r in reads:
            self.readers.setdefault(r, []).append(o["idx"])
        for r in writes:
            self.last_w[r] = o["idx"]
            self.readers[r] = []
        self.ops.append(o)
        self.per_eng[eng].append(o)
        return o["idx"]

    def dma(self, eng, out, in_, reads=(), writes=()):
        return self.op(eng, lambda e: e.dma_start(out=out, in_=in_), reads, writes, dma=True)

    def mm(self, out, lhsT, rhs, start, stop, reads=(), writes=()):
        return self.op("pe", lambda e: e.matmul(out, lhsT, rhs, start=start, stop=stop), reads, writes)

    def tr(self, out, in_, ident, reads=(), writes=()):
        return self.op("pe", lambda e: e.transpose(out, in_, ident), reads, writes)

    def act(self, out, in_, func, reads=(), writes=(), eng="act", **kw):
        return self.op(eng, lambda e: e.activation(out, in_, func, **kw), reads, writes)

    def tt(self, eng, out, in0, in1, op, reads=(), writes=()):
        return self.op(eng, lambda e: e.tensor_tensor(out, in0, in1, op), reads, writes)

    def ts(self, eng, out, in0, s1, s2, op0, op1, reads=(), writes=()):
        return self.op(eng, lambda e: e.tensor_scalar(out, in0, s1, s2, op0, op1), reads, writes)

    def tsmul(self, eng, out, in0, s1, reads=(), writes=()):
        return self.op(eng, lambda e: e.tensor_scalar_mul(out, in0, s1), reads, writes)

    def recip(self, out, in_, reads=(), writes=()):
        return self.op("dve", lambda e: e.reciprocal(out, in_), reads, writes)

    def stt(self, eng, out, in0, scalar, in1, op0, op1, reads=(), writes=()):
        return self.op(eng, lambda e: e.scalar_tensor_tensor(out, in0, scalar, in1, op0, op1), reads, writes)

    def copy(self, eng, out, in_, reads=(), writes=()):
        if eng == "act":
            return self.op(eng, lambda e: e.activation(out, in_, AF.Copy), reads, writes)
        return self.op(eng, lambda e: e.tensor_copy(out, in_), reads, writes)

    def memset(self, eng, ap, val, writes=()):
        return self.op(eng, lambda e: e.memset(ap, val), (), writes)

    def emit(self, stack):
        nc = self.nc
        ops = self.ops
        if not hasattr(self, "esem"):
            self.esem = {e: stack.enter_context(nc.semaphore("s_" + e)) for e in self.ENGS}
            self.dsem = [stack.enter_context(nc.semaphore("d%d" % i)) for i in range(self.ndma_sems)]
            self.dcount = [0] * self.ndma_sems
            self.dlast = [None] * self.ndma_sems
            self.rr = 0
            self.ecount = {e: 0 for e in self.ENGS}
            self.waited = {e: {} for e in self.ENGS}
            self.emitted = 0
        esem, dsem = self.esem, self.dsem
        new = ops[self.emitted:]
        for o in new:
            best = {}
            nd = set()
            for d in o["deps"]:
                p = ops[d]
                if p["dma"]:
                    nd.add(d)
                    continue
                if p["eng"] == "pe" and o["eng"] == "pe" and not o["dma"]:
                    continue
                if best.get(p["eng"], -1) < d:
                    best[p["eng"]] = d
            nd.update(best.values())
            o["deps"] = {d for d in nd if ops[d]["dma"] or d >= self.emitted}
        for o in new:
            if o["dma"] and o["own_sem"] is not None:
                o["sem"] = ("x", o["own_sem"], 16)
            elif o["dma"]:
                s = self.rr % self.ndma_sems
                self.rr += 1
                if self.dlast[s] is not None:
                    o["deps"].add(self.dlast[s])
                self.dcount[s] += 16
                o["sem"] = ("d", s, self.dcount[s])
                self.dlast[s] = o["idx"]
        for o in new:
            for d in o["deps"]:
                if not ops[d]["dma"]:
                    ops[d]["signal"] = True
        lastop = {}
        for o in new:
            if not o["dma"]:
                lastop[o["eng"]] = o
        for o in lastop.values():
            o["signal"] = True
        prev_end = dict(getattr(self, "phase_end", {}))
        for o in new:
            if (not o["dma"]) and o["signal"]:
                self.ecount[o["eng"]] += 1
                o["sem"] = ("e", o["eng"], self.ecount[o["eng"]])
        self.phase_end = dict(self.ecount)
        final_waits = [(dsem[s], self.dcount[s]) for s in range(self.ndma_sems) if self.dcount[s] > 0]
        start = self.emitted
        self.emitted = len(ops)
        with nc.Block() as blk:

            def run_engine(ename):
                def body(eng):
                    waited = self.waited[ename]
                    self.rank_cache = {}
                    for pe_, pv_ in prev_end.items():
                        if pv_ > 0 and waited.get(("e", pe_), 0) < pv_:
                            eng.wait_ge(esem[pe_], pv_)
                            waited[("e", pe_)] = pv_
                    for o in self.per_eng[ename]:
                        if o["idx"] < start:
                            continue
                        need = {}
                        for d in o["deps"]:
                            kind, key, val = ops[d]["sem"]
                            k = (kind, key)
                            if need.get(k, 0) < val:
                                need[k] = val
                        for k, val in need.items():
                            if waited.get(k, 0) >= val:
                                continue
                            eng.wait_ge(esem[k[1]] if k[0] == "e" else (dsem[k[1]] if k[0] == "d" else k[1]), val)
                            waited[k] = val
                        ins = o["fn"](eng)
                        if o["dma"] and o["own_sem"] is not None:
                            ins.then_inc(o["own_sem"], 16)
                        elif o["dma"]:
                            ins.then_inc(dsem[o["sem"][1]], 16)
                        elif o["signal"]:
                            if o["nosignal"]:
                                ins = eng.memset(self.nosig_scratch[:, self.nosig_n:self.nosig_n + 1], 0.0)
                                self.nosig_n += 1
                            ins.then_inc(esem[ename], 1)
                    if ename == "sp":
                        for sem, val in final_waits:
                            eng.wait_ge(sem, val)
                return body

            blk.tensor(run_engine("pe"))
            blk.scalar(run_engine("act"))
            blk.vector(run_engine("dve"))
            blk.gpsimd(run_engine("pool"))
            blk.sync(run_engine("sp"))

    def rank4(self, e):
        if "r" not in self.rank_cache:
            self.rank_cache["r"] = e.snap(e.partition_id() % 4, min_val=0, max_val=3)
        return self.rank_cache["r"]

    def end_phase(self, stack):
        last = {}
        for o in self.ops[getattr(self, "emitted", 0):]:
            if not o["dma"]:
                last[o["eng"]] = o
        self.emit(stack)


def _bf(a):
    return np.ascontiguousarray(a.astype(np.float32)).astype(NPBF)


def make_consts():
    c = {}
    c["ident"] = _bf(np.eye(128))
    k = np.arange(128)
    ang = 2 * np.pi * np.outer(k, k) / 128.0
    c["c1s1"] = _bf(np.concatenate([np.cos(ang), -np.sin(ang)], axis=1))
    c["ccsc"] = _bf(np.concatenate([np.cos(ang), np.sin(ang)], axis=1) / 1024.0)
    s2 = np.arange(64)[:, None, None]
    k1 = np.arange(128)[None, :, None]
    k2 = np.arange(64)[None, None, :]
    th = 2 * np.pi * ((s2 * (k1 + 128 * k2)) % 8192) / 8192.0
    c["w2a"] = _bf(np.concatenate([np.cos(th), -np.sin(th)], axis=2).reshape(64, 128 * 128))
    c["w2b"] = _bf(np.concatenate([np.sin(th), np.cos(th)], axis=2).reshape(64, 128 * 128))
    p0 = np.zeros((128, 128), np.float32)
    for i in range(64):
        p0[2 * i + 1, 2 * i] = -1.0
        p0[2 * i, 2 * i + 1] = 1.0
    c["p0"] = _bf(p0)
    c["ones"] = _bf(np.ones((128, 128)))
    return c


def rope_tables(core):
    q = core % 4
    s = np.arange(T) + q * T
    row = (s // 64).astype(np.float32)
    col = (s % 64).astype(np.float32)
    inv = (np.float32(10000.0) ** (-np.arange(32, dtype=np.float32) / np.float32(32))).astype(np.float32)
    ang = np.concatenate([row[:, None] * inv[None, :], col[:, None] * inv[None, :]], axis=1).astype(np.float32)
    cos = np.cos(ang).astype(np.float32)
    sin = np.sin(ang).astype(np.float32)
    cosT = np.repeat(cos.T, 2, axis=0)
    sinT = np.repeat(sin.T, 2, axis=0)
    return np.ascontiguousarray(cosT), np.ascontiguousarray(sinT)


def rmsnorm_to_hT(P, nc, st, xsrc_tile, xres_tok, ntiles, hT, hT_tok, ident, tag, load_fn=None, after_tile=None):
    sb = lambda n, s, d: st.enter_context(nc.sbuf_tensor("s_" + n, s, d))
    junk = [sb(f"{tag}_junk{j}", [128, D], BF16) for j in range(2)]
    xn = [sb(f"{tag}_xn{j}", [128, D], BF16) for j in range(2)]
    ss = sb(f"{tag}_ss", [128, ntiles], F32)
    rstd = sb(f"{tag}_rstd", [128, ntiles], F32)
    epsr = sb(f"{tag}_eps", [128, 1], F32)
    pst = [st.enter_context(nc.psum_tensor(f"p_{tag}_pst{j}", [128, 8, 128], BF16)) for j in range(2)]
    P.memset("dve", ss[:], 0.0, writes=[f"{tag}_ss"])
    P.memset("dve", epsr[:], EPS, writes=[f"{tag}_eps"])

    def stage1(i):
        if load_fn is not None:
            load_fn(i)
        j = i % 2
        xs = xsrc_tile(i)
        P.act(junk[j][:], xs, AF.Square, reads=[xres_tok(i), f"{tag}_ss"], writes=[f"{tag}_junk{j}", f"{tag}_ss{i}"],
              accum_out=ss[:, i:i + 1])
        P.act(rstd[:, i:i + 1], ss[:, i:i + 1], AF.Ln, reads=[f"{tag}_ss{i}", f"{tag}_eps"], writes=[f"{tag}_rstd{i}a"],
              scale=1.0 / D, bias=epsr[:, 0:1])
        P.act(rstd[:, i:i + 1], rstd[:, i:i + 1], AF.Exp, reads=[f"{tag}_rstd{i}a"], writes=[f"{tag}_rstd{i}"], scale=-0.5)

    def stage2(i):
        j = i % 2
        xs = xsrc_tile(i)
        P.act(xn[j][:], xs, AF.Copy, reads=[xres_tok(i), f"{tag}_rstd{i}"], writes=[f"{tag}_xn{j}"],
              scale=rstd[:, i:i + 1])
        for k in range(8):
            P.tr(pst[j][:, k, :], xn[j][:, k * 128:(k + 1) * 128], ident[:],
                 reads=[f"{tag}_xn{j}", "ident"], writes=[f"{tag}_pst{j}"])
        P.copy("dve", hT(i), pst[j][:], reads=[f"{tag}_pst{j}"], writes=[hT_tok(i)])

    for i in range(ntiles + 2):
        if i < ntiles:
            stage1(i)
        if 1 <= i <= ntiles:
            stage2(i - 1)
        if i >= 2 and after_tile is not None:
            after_tile(i - 2)


def build(mode):
    nc = bass.Bass("TRN2", target_bir_lowering=False)
    dr = lambda n, s, d, k: nc.dram_tensor(n, s, d, kind=k).ap()
    IN, OUT = "ExternalInput", "ExternalOutput"
    with ExitStack() as st:
        sb = lambda n, s, d: st.enter_context(nc.sbuf_tensor("s_" + n, s, d))
        ps = lambda n, s, d=F32: st.enter_context(nc.psum_tensor("p_" + n, s, d))
        P = Prog(nc)
        if mode == "A":
            phaseA(P, nc, st, sb, ps, dr, IN, OUT)
        elif mode == "B":
            phaseB(P, nc, st, sb, ps, dr, IN, OUT)
        elif mode == "C":
            phaseC(P, nc, st, sb, ps, dr, IN, OUT)
        elif mode == "E":
            phaseE(P, nc, st, sb, ps, dr, IN, OUT)
        elif mode == "D":
            phaseD(P, nc, st, sb, ps, dr, IN, OUT)
        P.emit(st)
    return nc


def stream_w_chunks(P, nc, sb, w_dram, gfull, ncols, order, tag, nstage=3, nbuf=3, engs=("dve", "pool"), eager=False, ahead=None):
    stage = [sb(f"{tag}_stg{j}", [128, 8, 128], F32) for j in range(nstage)]
    wb = [sb(f"{tag}_wb{j}", [128, 8, 128], BF16) for j in range(nbuf)]
    wv = w_dram.rearrange("(k p) n -> p k n", p=128)
    issued = [0]

    def issue(n):
        cid = order[n]
        sj, bj = n % nstage, n % nbuf
        P.dma("sp", stage[sj][:], wv[:, :, cid * 128:(cid + 1) * 128], writes=[f"{tag}_stg{sj}"])
        eng = engs[n % len(engs)]
        P.tt(eng, wb[bj][:], stage[sj][:], gfull[:], ALU.mult, reads=[f"{tag}_stg{sj}", "gfull"], writes=[f"{tag}_wb{bj}"])

    for n, cid in enumerate(order):
        if ahead is None:
            ahead = (nbuf - 1) if eager else 0
        while issued[0] <= min(n + ahead, len(order) - 1):
            issue(issued[0])
            issued[0] += 1
        yield (cid, wb[n % nbuf], f"{tag}_wb{n % nbuf}")


def phaseA(P, nc, st, sb, ps, dr, IN, OUT):
    x = dr("x", [T, D], F32, IN)
    xh = dr("xh", [2, D], F32, IN)
    g0 = dr("g0", [128, 8], F32, IN)
    w_in = dr("w_in", [D, 3072], F32, IN)
    cw = dr("cw", [128, 12], F32, IN)
    identd = dr("ident", [128, 128], BF16, IN)
    u_out = dr("u", [T, 512], BF16, OUT)
    ya_out = dr("yaT", [4, 128, T], BF16, OUT)
    sbz_out = dr("sbzT", [4, 128, T], BF16, OUT)

    ident = sb("ident", [128, 128], BF16)
    g0s = sb("g0s", [128, 8], F32)
    cws = sb("cws", [128, 12], F32)
    gfull = sb("gfull", [128, 8, 128], F32)
    hT = sb("hT", [128, 8, T], BF16)
    hTh = sb("hTh", [128, 8, 128], BF16)
    xt = [sb(f"xt{j}", [128, D], F32) for j in range(3)]
    P.dma("sp", ident[:], identd, writes=["ident"])
    P.dma("sp", g0s[:], g0, writes=["g0s"])
    P.dma("sp", cws[:], cw, writes=["cws"])
    P.memset("pool", gfull[:], 1.0, writes=["gfull"])
    for k in range(8):
        P.tsmul("pool", gfull[:, k, :], gfull[:, k, :], g0s[:, k:k + 1], reads=["g0s", "gfull"], writes=["gfull"])

    xv = x.rearrange("(i p) d -> i p d", p=128)

    def load(i):
        j = i % 3
        if i < NT:
            P.dma("sp", xt[j][:], xv[i], writes=[f"xt{j}"])
        else:
            P.memset("pool", xt[j][:], 0.0, writes=[f"xt{j}"])
            P.dma("sp", xt[j][0:2, :], xh, writes=[f"xt{j}"])

    if True:
        rmsnorm_to_hT(P, nc, st, lambda i: xt[i % 3][:], lambda i: f"xt{i % 3}", NT + 1,
                      lambda i: (hT[:, :, i * 128:(i + 1) * 128] if i < NT else hTh[:, :, :]),
                      lambda i: (f"hT{i // 4}" if i < NT else "hTh"), ident, "rn", load_fn=load)

        order = [16, 17, 18, 19]
        for j in range(4):
            order += [j, 8 + j, 4 + j, 12 + j]
        order += [20, 21, 22, 23]
        pp = [ps(f"pp{j}", [128, 512]) for j in range(4)]
        ph = ps("ph", [128, 2, 2])
        u_sb = sb("u_sb", [128, NT, 512], BF16)
        tbuf = sb("tbuf", [128, T + 2], F32)
        cbuf = sb("cbuf", [128, T], F32)
        axs = [sb(f"axs{j}", [128, 512], F32) for j in range(2)]
        szs = [sb(f"szs{j}", [128, 512], F32) for j in range(2)]
        vs = [sb(f"vs{j}", [128, 512], F32) for j in range(2)]
        hprod = sb("hprod", [128, 2, 2], F32)
        yaT = sb("yaT", [128, 4, T], BF16)
        sbzT = sb("sbzT", [128, 4, T], BF16)
        ppi = [0]

        def proj(wb, wtok, tb):
            j = ppi[0] % 4
            ppi[0] += 1
            for k in range(8):
                P.mm(pp[j][:], wb[:, k, :], hT[:, k, tb * 512:(tb + 1) * 512], k == 0, k == 7,
                     reads=[wtok, f"hT{tb}"], writes=[f"pp{j}"])
            return pp[j], f"pp{j}"

        chunks = {}
        n_u = 0
        for cid, wb, wtok in stream_w_chunks(P, nc, sb, w_in, gfull, 3072, order, "wi"):
            kind, j = cid // 4, cid % 4
            if kind == 4:
                for i in range(NT):
                    pj = ppi[0] % 4
                    ppi[0] += 1
                    for k in range(8):
                        P.mm(pp[pj][:, 0:128], hT[:, k, i * 128:(i + 1) * 128], wb[:, k, :], k == 0, k == 7,
                             reads=[wtok, f"hT{i // 4}"], writes=[f"pp{pj}"])
                    P.copy("act", u_sb[:, i, j * 128:(j + 1) * 128], pp[pj][:, 0:128], reads=[f"pp{pj}"], writes=[f"u_sb{j}"])
                if j == 3:
                    P.dma("act", u_out.rearrange("(i p) c -> p i c", p=128), u_sb[:], reads=[f"u_sb{jj}" for jj in range(4)])
            elif kind == 5:
                for tb in range(4):
                    pt, ptok = proj(wb, wtok, tb)
                    P.act(sbzT[:, j, tb * 512:(tb + 1) * 512], pt[:], AF.Silu, reads=[ptok], writes=[f"sbzT{j}"])
                P.dma("act", sbz_out[j], sbzT[:, j, :], reads=[f"sbzT{j}"])
            elif kind == 0:
                chunks["x"] = (wb, wtok)
            elif kind == 2:
                wbx, wtokx = chunks["x"]
                for tb in range(4):
                    ptx, ptokx = proj(wbx, wtokx, tb)
                    a = tb % 2
                    P.copy("act", axs[a][:], ptx[:], reads=[ptokx], writes=[f"axs{a}"])
                    ptc, ptokc = proj(wb, wtok, tb)
                    P.tt("dve", tbuf[:, 1 + tb * 512:1 + (tb + 1) * 512], ptc[:], axs[a][:], ALU.mult,
                         reads=[ptokc, f"axs{a}"], writes=[f"tbuf{tb}"])
                for k in range(8):
                    P.mm(ph[:, 0, :], wbx[:, k, :], hTh[:, k, 0:2], k == 0, k == 7, reads=[wtokx, "hTh"], writes=["ph0"])
                for k in range(8):
                    P.mm(ph[:, 1, :], wb[:, k, :], hTh[:, k, 0:2], k == 0, k == 7, reads=[wtok, "hTh"], writes=["ph1"])
                P.copy("act", hprod[:, 0, :], ph[:, 0, :], reads=["ph0"], writes=["hprod0"])
                P.tt("dve", hprod[:, 1, :], ph[:, 1, :], hprod[:, 0, :], ALU.mult, reads=["ph1", "hprod0"], writes=["hprod1"])
                P.copy("dve", tbuf[:, 0:1], hprod[:, 1, 0:1], reads=["hprod1"], writes=["tbufL"])
                P.copy("dve", tbuf[:, T + 1:T + 2], hprod[:, 1, 1:2], reads=["hprod1"], writes=["tbufR"])
                alltb = [f"tbuf{tb}" for tb in range(4)]
                P.act(cbuf[:], tbuf[:, 1:T + 1], AF.Copy, reads=alltb, writes=["cbuf"], scale=cws[:, 3 * j + 1:3 * j + 2])
                P.stt("dve", cbuf[:], tbuf[:, 0:T], cws[:, 3 * j:3 * j + 1], cbuf[:], ALU.mult, ALU.add,
                      reads=alltb + ["tbufL", "cws"], writes=["cbuf"])
                P.stt("dve", cbuf[:], tbuf[:, 2:T + 2], cws[:, 3 * j + 2:3 * j + 3], cbuf[:], ALU.mult, ALU.add,
                      reads=alltb + ["tbufR", "cws"], writes=["cbuf"])
            elif kind == 1:
                chunks["b"] = (wb, wtok)
            elif kind == 3:
                wbb, wtokb = chunks["b"]
                for tb in range(4):
                    a = tb % 2
                    ptz, ptokz = proj(wb, wtok, tb)
                    P.act(szs[a][:], ptz[:], AF.Silu, reads=[ptokz], writes=[f"szs{a}"])
                    ptb, ptokb = proj(wbb, wtokb, tb)
                    P.tt("dve", vs[a][:], ptb[:], szs[a][:], ALU.mult, reads=[ptokb, f"szs{a}"], writes=[f"vs{a}"])
                    P.tt("pool", yaT[:, j, tb * 512:(tb + 1) * 512], cbuf[:, tb * 512:(tb + 1) * 512], vs[a][:], ALU.mult,
                         reads=["cbuf", f"vs{a}"], writes=[f"yaT{j}"])
                P.dma("pool", ya_out[j], yaT[:, j, :], reads=[f"yaT{j}"])


def halo_rows(x, core):
    b, q = core // 4, core % 4
    lo, hi = q * T, (q + 1) * T
    xh = np.zeros((2, D), np.float32)
    if lo > 0:
        xh[0] = x[b, lo - 1]
    if hi < S:
        xh[1] = x[b, hi]
    return xh


def inputs_A(inp):
    c = make_consts()
    x = inp["x"]
    g0 = np.ascontiguousarray(inp["norm_even"][0].reshape(8, 128).T)
    cw = np.ascontiguousarray(inp["conv_w"][0].reshape(3, 4, 128).transpose(2, 1, 0).reshape(128, 12))
    w_in = np.ascontiguousarray(inp["w_in_even"][0])
    maps = []
    for core in range(NCORES):
        b, q = core // 4, core % 4
        maps.append(dict(x=np.ascontiguousarray(x[b, q * T:(q + 1) * T]), xh=halo_rows(x, core), g0=g0,
                         w_in=w_in, cw=cw, ident=c["ident"]))
    return maps


def fft_consts_packed(c):
    w2 = np.concatenate([np.asarray(c["w2a"]), np.asarray(c["w2b"])], axis=0)
    return np.ascontiguousarray(w2)


def phaseB(P, nc, st, sb, ps, dr, IN, OUT):
    u = dr("ubg", [S, 128], BF16, IN)
    c1s1d = dr("c1s1", [128, 256], BF16, IN)
    w2d = dr("w2", [128, 128 * 128], BF16, IN)
    ccscd = dr("ccsc", [128, 256], BF16, IN)
    ft = dr("ft", [128, S], F32, OUT)
    fft_core(P, nc, sb, ps, u, c1s1d, w2d, ccscd, ft)


def fft_core(P, nc, sb, ps, u, c1s1d, w2d, ccscd, ft):
    X0 = sb("X0", [128, 2, 64, 128], BF16)
    X1 = sb("X1", [128, 128, 128], BF16)
    W2 = sb("W2", [128, 128, 128], BF16)
    X2 = sb("X2", [128, 2, S], BF16)
    X2r = X2[:, 0, :]
    X2i = X2[:, 1, :]
    c1s1 = sb("c1s1", [128, 256], BF16)
    ccsc = sb("ccsc", [128, 256], BF16)
    fo = [sb(f"fo{j}", [128, 2048], F32) for j in range(2)]
    pb = [ps(f"pb{j}", [128, 512]) for j in range(4)]
    uv = u.rearrange("(p j) c -> p j c", j=64)
    P.dma("sp", c1s1[:], c1s1d, writes=["c1s1"])
    P.dma("sp", X0[:, 0, :, :], uv, writes=["X0a"])
    P.dma("pool", X0[:, 1, :, :], uv, writes=["X0b"])
    P.dma("sp", ccsc[:], ccscd, writes=["ccsc"])
    w2v = w2d.rearrange("p (a b) -> p a b", b=128)
    for h in range(4):
        P.dma("sp" if h % 2 == 0 else "pool", W2[:, h * 32:(h + 1) * 32, :], w2v[:, h * 32:(h + 1) * 32, :], writes=[f"W2_{h}"])
    n = 0
    for c0 in range(0, int(os.environ.get("FFT_NC", "128")), 2):
        j = n % 4
        n += 1
        pt = pb[j][:].rearrange("p (a b) -> p a b", b=256)
        for cc in range(2):
            c = c0 + cc
            P.mm(pt[:, cc, :], X0[:, :, :, c], c1s1[:], True, True, reads=["X0a", "X0b", "c1s1"], writes=[f"pb{j}"])
        if os.environ.get("FFT_NOCOPY") == "1":
            continue
        P.copy("dve", X1[0:64, c0:c0 + 2, :], pt[0:64, :, 0:128], reads=[f"pb{j}"], writes=[f"X1r{c0}"])
        if os.environ.get("FFT_NOCOPY") == "2":
            continue
        P.copy("act", X1[64:128, c0:c0 + 2, :], pt[64:128, :, 128:256], reads=[f"pb{j}"], writes=[f"X1i{c0}"])
    if os.environ.get("FFT_STEPS") == "1":
        P.dma("sp", ft[:, 0:2048], fo[0][:], reads=[f"X1r{c0}" for c0 in range(0, 128, 2)] + [f"X1i{c0}" for c0 in range(0, 128, 2)])
        return
    x1all = [f"X1r{c0}" for c0 in range(0, 128, 2)] + [f"X1i{c0}" for c0 in range(0, 128, 2)]
    for k0 in range(0, int(os.environ.get("FFT_NK", "128")), 4):
        j = n % 4
        n += 1
        pt = pb[j][:].rearrange("p (a b) -> p a b", b=128)
        for kk in range(4):
            k1 = k0 + kk
            P.mm(pt[:, kk, :], X1[:, :, k1], W2[:, k1, :], True, True,
                 reads=x1all + [f"W2_{k1 // 32}"], writes=[f"pb{j}"])
        X2v = X2[:].rearrange("p r (k1 k2) -> p k1 r k2", k2=64)[:, k0:k0 + 4, :, :]
        P.copy("dve" if (k0 // 4) % 2 == 0 else "act", X2v, pb[j][:].rearrange("p (a r b) -> p a r b", r=2, b=64),
               reads=[f"pb{j}"], writes=[f"X2r{k0}", f"X2i{k0}"])
    if os.environ.get("FFT_STEPS") == "2":
        P.dma("sp", ft[:, 0:2048], fo[0][:], reads=[f"X2r{k0}" for k0 in range(0, 128, 4)] + [f"X2i{k0}" for k0 in range(0, 128, 4)])
        return
    x2all = [f"X2r{k0}" for k0 in range(0, 128, 4)] + [f"X2i{k0}" for k0 in range(0, 128, 4)]
    for kb in range(16):
        j = n % 4
        n += 1
        rr_ = X2r.rearrange("p (k1 k2) -> p k2 k1", k2=64)[:, 4 * kb:4 * kb + 4, :]
        ri_ = X2i.rearrange("p (k1 k2) -> p k2 k1", k2=64)[:, 4 * kb:4 * kb + 4, :]
        P.mm(pb[j][:], ccsc[:, 0:128], rr_, True, False, reads=x2all + ["ccsc"], writes=[f"pb{j}"])
        P.mm(pb[j][:], ccsc[:, 128:256], ri_, False, True, reads=x2all + ["ccsc"], writes=[f"pb{j}"])
        f = kb // 4
        P.copy("dve" if kb % 2 == 0 else "act", fo[f % 2][:, (kb % 4) * 512:(kb % 4 + 1) * 512], pb[j][:],
               reads=[f"pb{j}"], writes=[f"fo{f % 2}_{kb % 4}"])
        if kb % 4 == 3:
            P.dma("sp", ft[:, f * 2048:(f + 1) * 2048], fo[f % 2][:], reads=[f"fo{f % 2}_{q}" for q in range(4)])


def inputs_B(u_full):
    c = make_consts()
    w2 = fft_consts_packed(c)
    maps = []
    for core in range(NCORES):
        b, g = core // 4, core % 4
        maps.append(dict(ubg=np.ascontiguousarray(u_full[b, :, g * 128:(g + 1) * 128]), c1s1=c["c1s1"], w2=w2, ccsc=c["ccsc"]))
    return maps


def phaseC(P, nc, st, sb, ps, dr, IN, OUT):
    x = dr("x", [T, D], F32, IN)
    ftd = dr("ft", [4, 128, T], F32, IN)
    yad = dr("yaT", [4, 128, T], BF16, IN)
    sbzd = dr("sbzT", [4, 128, T], BF16, IN)
    wod = dr("w_out", [D, D], F32, IN)
    x1 = dr("x1", [T, D], F32, OUT)
    yT = sb("yT", [128, 8, T], BF16)
    sbz = sb("sbz", [128, 4, T], BF16)
    fst = [sb(f"fst{j}", [128, T], BF16) for j in range(2)]
    Wo = sb("Wo", [128, 8, D], BF16)
    wst = [sb(f"wst{j}", [128, D], F32) for j in range(2)]
    xt = [sb(f"xt{j}", [128, D], F32) for j in range(3)]
    pso = [ps(f"pso{j}", [128, 512]) for j in range(4)]
    for g in range(4):
        P.dma("sp", yT[:, g, :], yad[g], writes=[f"yT{g}"])
        P.dma("pool", sbz[:, g, :], sbzd[g], writes=[f"sbz{g}"])
    for g in range(4):
        P.dma("sp", fst[g % 2][:], ftd[g], writes=[f"fst{g % 2}"])
        P.tt("dve" if g % 2 == 0 else "pool", yT[:, 4 + g, :], fst[g % 2][:], sbz[:, g, :], ALU.mult,
             reads=[f"fst{g % 2}", f"sbz{g}"], writes=[f"yT{4 + g}"])
    for kc in range(8):
        P.dma("sp", wst[kc % 2][:], wod[kc * 128:(kc + 1) * 128, :], writes=[f"wst{kc % 2}"])
        P.copy("dve" if kc % 2 == 0 else "pool", Wo[:, kc, :], wst[kc % 2][:], reads=[f"wst{kc % 2}"], writes=[f"Wo{kc}"])
    xv = x.rearrange("(i p) d -> i p d", p=128)
    x1v = x1.rearrange("(i p) d -> i p d", p=128)
    n = 0
    for i in range(NT):
        j3 = i % 3
        P.dma("sp", xt[j3][:], xv[i], writes=[f"xt{j3}"])
        for nb in range(2):
            j = n % 4
            n += 1
            for kc in range(8):
                P.mm(pso[j][:], yT[:, kc, i * 128:(i + 1) * 128], Wo[:, kc, nb * 512:(nb + 1) * 512], kc == 0, kc == 7,
                     reads=[f"yT{kc}", f"Wo{kc}"], writes=[f"pso{j}"])
            P.tt("dve", xt[j3][:, nb * 512:(nb + 1) * 512], pso[j][:], xt[j3][:, nb * 512:(nb + 1) * 512], ALU.add,
                 reads=[f"pso{j}", f"xt{j3}"], writes=[f"xt{j3}"])
        P.dma("act", x1v[i], xt[j3][:], reads=[f"xt{j3}"])


def inputs_C(inp, ft_full, ya, sbz):
    maps = []
    w_out = np.ascontiguousarray(inp["w_out_even"][0])
    for core in range(NCORES):
        b, q = core // 4, core % 4
        ft = np.ascontiguousarray(ft_full[b * 4:(b + 1) * 4, :, q * T:(q + 1) * T])
        maps.append(dict(x=np.ascontiguousarray(inp["x"][b, q * T:(q + 1) * T]), ft=ft, yaT=ya[core], sbzT=sbz[core], w_out=w_out))
    return maps


def phaseE(P, nc, st, sb, ps, dr, IN, OUT):
    x1 = dr("x1", [T, D], F32, IN)
    g1 = dr("g1", [128, 8], F32, IN)
    w_in = dr("w_in", [D, 2560], F32, IN)
    gv = dr("gv", [128, 4], F32, IN)
    cosd = dr("cos", [128, T], F32, IN)
    sind = dr("sin", [128, T], F32, IN)
    identd = dr("ident", [128, 128], BF16, IN)
    p0d = dr("p0", [128, 128], BF16, IN)
    onesd = dr("ones", [128, 128], BF16, IN)
    qT_out = dr("qT", [8, 128, T], BF16, OUT)
    kT_out = dr("kT", [2, 128, T], BF16, OUT)
    vx_out = dr("vx", [T, 258], BF16, OUT)
    sz_out = dr("szT", [8, 128, T], BF16, OUT)

    ident = sb("ident", [128, 128], BF16)
    p0 = sb("p0", [128, 128], BF16)
    ones = sb("ones", [128, 128], BF16)
    g1s = sb("g1s", [128, 8], F32)
    gvs = sb("gvs", [128, 4], F32)
    gfull = sb("gfull", [128, 8, 128], F32)
    cos = sb("cos", [128, T], F32)
    sin = sb("sin", [128, T], F32)
    hT = sb("hT", [128, 8, T], BF16)
    xt = [sb(f"xt{j}", [128, D], F32) for j in range(3)]
    for t_, d_, n_ in ((ident, identd, "ident"), (p0, p0d, "p0"), (ones, onesd, "ones"), (g1s, g1, "g1s"), (gvs, gv, "gvs"),
                       (cos, cosd, "cos"), (sin, sind, "sin")):
        P.dma("sp", t_[:], d_, writes=[n_])
    P.memset("pool", gfull[:], 1.0, writes=["gfull"])
    for k in range(8):
        P.tsmul("pool", gfull[:, k, :], gfull[:, k, :], g1s[:, k:k + 1], reads=["g1s", "gfull"], writes=["gfull"])
    xv = x1.rearrange("(i p) d -> i p d", p=128)

    def load(i):
        P.dma("sp", xt[i % 3][:], xv[i], writes=[f"xt{i % 3}"])

    rmsnorm_to_hT(P, nc, st, lambda i: xt[i % 3][:], lambda i: f"xt{i % 3}", NT,
                  lambda i: hT[:, :, i * 128:(i + 1) * 128], lambda i: f"hT{i // 4}", ident, "rn", load_fn=load)

    pp = [ps(f"pp{j}", [128, 512]) for j in range(4)]
    pss = ps("pss", [128, 512])
    prot = ps("prot", [128, 512])
    qb = [sb(f"qb{j}", [128, 512], BF16) for j in range(2)]
    sq = [sb(f"sq{j}", [128, 512], BF16) for j in range(2)]
    rs = [sb(f"rs{j}", [128, 512], F32) for j in range(2)]
    ta = [sb(f"ta{j}", [128, 512], F32) for j in range(2)]
    tb_ = [sb(f"tb{j}", [128, 512], F32) for j in range(2)]
    obuf = [sb(f"obuf{j}", [128, T], BF16) for j in range(2)]
    vx_sb = sb("vx_sb", [128, NT, 2, 129], BF16)
    P.memset("pool", vx_sb[:], 1.0, writes=["vx_ones"])
    ppi = [0]
    cnt = [0]
    nob = [0]

    def proj(wb, wtok, tb):
        j = ppi[0] % 4
        ppi[0] += 1
        for k in range(8):
            P.mm(pp[j][:], wb[:, k, :], hT[:, k, tb * 512:(tb + 1) * 512], k == 0, k == 7,
                 reads=[wtok, f"hT{tb}"], writes=[f"pp{j}"])
        return pp[j], f"pp{j}"

    order = [8, 9, 10, 11] + list(range(8)) + list(range(12, 20))
    for cid, wb, wtok in stream_w_chunks(P, nc, sb, w_in, gfull, 2560, order, "wi"):
        if cid < 10:
            isq = cid < 8
            gcol = 0 if isq else 2
            ob = nob[0] % 2
            nob[0] += 1
            for tb in range(4):
                a = cnt[0] % 2
                cnt[0] += 1
                pt, ptok = proj(wb, wtok, tb)
                sl = slice(tb * 512, (tb + 1) * 512)
                P.copy("act", qb[a][:], pt[:], reads=[ptok], writes=[f"qb{a}", f"lk1_{a}"])
                P.act(sq[a][:], pt[:], AF.Square, reads=[ptok], writes=[f"sq{a}", f"lk2_{a}"])
                P.mm(pss[:], ones[:], sq[a][:], True, True, reads=["ones", f"sq{a}"], writes=["pss"])
                P.mm(prot[:], p0[:], qb[a][:], True, True, reads=["p0", f"qb{a}"], writes=["prot"])
                P.ts("dve", rs[a][:], pss[:], 1.0 / 128, EPS, ALU.mult, ALU.add, reads=["pss"], writes=[f"rs{a}"])
                P.act(rs[a][:], rs[a][:], AF.Sqrt, reads=[f"rs{a}"], writes=[f"rs{a}"])
                P.recip(rs[a][:], rs[a][:], reads=[f"rs{a}"], writes=[f"rs{a}"])
                P.stt("dve", ta[a][:], pt[:], gvs[:, gcol:gcol + 1], cos[:, sl], ALU.mult, ALU.mult,
                      reads=[ptok, f"lk1_{a}", f"lk2_{a}", "gvs", "cos"], writes=[f"ta{a}"])
                P.stt("dve", tb_[a][:], prot[:], gvs[:, gcol + 1:gcol + 2], sin[:, sl], ALU.mult, ALU.mult,
                      reads=["prot", "gvs", "sin"], writes=[f"tb{a}"])
                P.tt("pool", ta[a][:], ta[a][:], tb_[a][:], ALU.add, reads=[f"ta{a}", f"tb{a}"], writes=[f"ta{a}"])
                P.tt("pool", obuf[ob][:, sl], ta[a][:], rs[a][:], ALU.mult, reads=[f"ta{a}", f"rs{a}"], writes=[f"obuf{ob}"])
            dst = qT_out[cid] if isq else kT_out[cid - 8]
            P.dma("pool", dst, obuf[ob][:], reads=[f"obuf{ob}"])
        elif cid < 12:
            kvh = cid - 10
            for i in range(NT):
                j = ppi[0] % 4
                ppi[0] += 1
                for k in range(8):
                    P.mm(pp[j][:, 0:128], hT[:, k, i * 128:(i + 1) * 128], wb[:, k, :], k == 0, k == 7,
                         reads=[wtok, f"hT{i // 4}"], writes=[f"pp{j}"])
                P.copy("act", vx_sb[:, i, kvh, 0:128], pp[j][:, 0:128], reads=[f"pp{j}", "vx_ones"], writes=[f"vx{kvh}"])
            if kvh == 1:
                P.dma("act", vx_out.rearrange("(i p) c -> p i c", p=128), vx_sb[:].rearrange("p i a b -> p i (a b)"),
                      reads=["vx0", "vx1"])
        else:
            ob = nob[0] % 2
            nob[0] += 1
            for tb in range(4):
                pt, ptok = proj(wb, wtok, tb)
                P.act(obuf[ob][:, tb * 512:(tb + 1) * 512], pt[:], AF.Silu, reads=[ptok], writes=[f"obuf{ob}"])
            P.dma("act", sz_out[cid - 12], obuf[ob][:], reads=[f"obuf{ob}"])


def inputs_E(inp, x1_cores):
    c = make_consts()
    g1 = np.ascontiguousarray(inp["norm_odd"][0].reshape(8, 128).T)
    gq = inp["q_gain"][0]
    gk = inp["k_gain"][0]
    partner = np.arange(128) ^ 1
    gv = np.ascontiguousarray(np.stack([gq, gq[partner], gk, gk[partner]], axis=1).astype(np.float32))
    w_in = np.ascontiguousarray(inp["w_in_odd"][0])
    maps = []
    for core in range(NCORES):
        cosT, sinT = rope_tables(core)
        maps.append(dict(x1=x1_cores[core], g1=g1, w_in=w_in, gv=gv, cos=cosT, sin=sinT,
                         ident=c["ident"], p0=c["p0"], ones=c["ones"]))
    return maps


def phaseD(P, nc, st, sb, ps, dr, IN, OUT):
    qTd = dr("qT", [8, 128, T], BF16, IN)
    kTd = dr("kTf", [2, 128, S], BF16, IN)
    vxd = dr("vxf", [S, 258], BF16, IN)
    szd = dr("szT", [8, 128, T], BF16, IN)
    x1 = dr("x1", [T, D], F32, IN)
    wod = dr("w_out", [D, D], F32, IN)
    fnbd = dr("fnb", [128, D], F32, IN)
    gbd = dr("gb", [128, 2, 128], F32, IN)
    identd = dr("ident", [128, 128], BF16, IN)
    out = dr("out", [T, D], F32, OUT)

    ident = sb("ident", [128, 128], BF16)
    gb = sb("gb", [128, 2, 128], F32)
    fnb = sb("fnb", [128, D], F32)
    mx = sb("mx", [128, 4], F32)
    kT = sb("kT", [128, 2, S], BF16)
    V = sb("V", [128, 64, 258], BF16)
    qT = sb("qT", [128, 8, T], BF16)
    szT = sb("szT", [128, 8, T], BF16)
    Wo = sb("Wo", [128, 8, D], BF16)
    wst = [sb(f"wst{j}", [128, D], F32) for j in range(2)]
    PT = [sb(f"PT{j}", [128, 2, 512], BF16) for j in range(3)]
    on = [sb(f"on{j}", [128, 128], BF16) for j in range(2)]
    rden = sb("rden", [128, 8], F32)
    xt = [sb(f"xt{j}", [128, D], F32) for j in range(2)]
    junk = sb("junk", [128, D], BF16)
    ss = sb("ss", [128, NT], F32)
    pS = [ps(f"pS{j}", [128, 2, 512]) for j in range(2)]
    pO = [ps(f"pO{j}", [128, 2, 129]) for j in range(2)]
    pT = ps("pT", [128, 4, 128], BF16)

    P.dma("sp", ident[:], identd, writes=["ident"])
    P.dma("sp", gb[:], gbd, writes=["gb"])
    P.dma("sp", fnb[:], fnbd, writes=["fnb"])
    for g in range(2):
        P.dma("sp", kT[:, g, :], kTd[g], writes=[f"kT{g}"])
    vv = vxd.rearrange("(c p) f -> p c f", p=128)
    for c4 in range(4):
        P.dma("pool", V[:, c4 * 16:(c4 + 1) * 16, :], vv[:, c4 * 16:(c4 + 1) * 16, :], writes=[f"V{c4}"])
    for h in range(8):
        P.dma("sp", qT[:, h, :], qTd[h], writes=[f"q{h}_{qb}" for qb in range(4)])
        P.dma("pool", szT[:, h, :], szd[h], writes=[f"sz{h}"])
    for kc in range(8):
        P.dma("sp", wst[kc % 2][:], wod[kc * 128:(kc + 1) * 128, :], writes=[f"wst{kc % 2}"])
        P.copy("pool", Wo[:, kc, :], wst[kc % 2][:], reads=[f"wst{kc % 2}"], writes=[f"Wo{kc}"])
    scale = 128.0 ** -0.5
    P.op("dve", lambda e: e.reduce_max(mx[:, 0:1], gb[:, 0, :], AX.X, apply_absolute_value=True), reads=["gb"], writes=["mx0"])
    P.op("dve", lambda e: e.reduce_max(mx[:, 1:2], gb[:, 1, :], AX.X, apply_absolute_value=True), reads=["gb"], writes=["mx1"])
    P.tt("dve", mx[:, 2:3], mx[:, 0:1], mx[:, 1:2], ALU.mult, reads=["mx0", "mx1"], writes=["mx2"])
    P.ts("dve", mx[:, 3:4], mx[:, 2:3], -scale * 128.0, 0.0, ALU.mult, ALU.add, reads=["mx2"], writes=["nbias"])

    Vv = V[:].rearrange("p c (g f) -> p c g f", g=2)
    nS = 0
    for h in range(8):
        g = h // 4
        for qb in range(4):
            qsl = slice(qb * 512, (qb + 1) * 512)
            for kc2 in range(32):
                j = nS % 2
                jp = nS % 3
                nS += 1
                for c2 in range(2):
                    kc = 2 * kc2 + c2
                    P.mm(pS[j][:, c2, :], kT[:, g, kc * 128:(kc + 1) * 128], qT[:, h, qsl], True, True,
                         reads=[f"kT{g}", f"q{h}_{qb}"], writes=[f"pS{j}"])
                P.act(PT[jp][:], pS[j][:], AF.Exp, reads=[f"pS{j}", "nbias"], writes=[f"PT{jp}"], scale=scale, bias=mx[:, 3:4])
                for c2 in range(2):
                    kc = 2 * kc2 + c2
                    for qt in range(4):
                        P.mm(pO[qt // 2][:, qt % 2, :], PT[jp][:, c2, qt * 128:(qt + 1) * 128], Vv[:, kc, g, :],
                             kc == 0, kc == 63, reads=[f"PT{jp}", f"V{kc // 16}"], writes=[f"pO{qt}"])
            for qt in range(4):
                a = qt % 2
                acc = pO[qt // 2][:, qt % 2, :]
                col = (h * 4 + qb) % 2 * 4 + qt
                P.recip(rden[:, col:col + 1], acc[:, 128:129], reads=[f"pO{qt}"], writes=[f"rden{col}"])
                P.tsmul("dve", on[a][:], acc[:, 0:128], rden[:, col:col + 1], reads=[f"pO{qt}", f"rden{col}"], writes=[f"on{a}"])
                P.tr(pT[:, qt, :], on[a][:], ident[:], reads=[f"on{a}", "ident"], writes=["pT"])
            P.tt("dve", qT[:, h, qsl], pT[:].rearrange("p a b -> p (a b)"), szT[:, h, qsl], ALU.mult,
                 reads=["pT", f"sz{h}"], writes=[f"q{h}_{qb}"])

    xv = x1.rearrange("(i p) d -> i p d", p=128)
    ov = out.rearrange("(i p) d -> i p d", p=128)
    P.memset("dve", ss[:], 0.0, writes=["ss"])
    for i in range(NT):
        j2 = i % 2
        P.dma("sp", xt[j2][:], xv[i], writes=[f"xt{j2}"])
        for nb in range(2):
            j = nS % 2
            nS += 1
            for h in range(8):
                P.mm(pS[j][:, 0, :], qT[:, h, i * 128:(i + 1) * 128], Wo[:, h, nb * 512:(nb + 1) * 512], h == 0, h == 7,
                     reads=[f"q{h}_{i // 4}", f"Wo{h}"], writes=[f"pS{j}"])
            P.tt("dve", xt[j2][:, nb * 512:(nb + 1) * 512], pS[j][:, 0, :], xt[j2][:, nb * 512:(nb + 1) * 512], ALU.add,
                 reads=[f"pS{j}", f"xt{j2}"], writes=[f"xt{j2}"])
        P.act(junk[:], xt[j2][:], AF.Square, reads=[f"xt{j2}", "ss"], writes=["junk", f"ss{i}"], accum_out=ss[:, i:i + 1])
        P.ts("dve", ss[:, i:i + 1], ss[:, i:i + 1], 1.0 / D, EPS, ALU.mult, ALU.add, reads=[f"ss{i}"], writes=[f"ss{i}"])
        P.act(ss[:, i:i + 1], ss[:, i:i + 1], AF.Sqrt, reads=[f"ss{i}"], writes=[f"ss{i}"])
        P.recip(ss[:, i:i + 1], ss[:, i:i + 1], reads=[f"ss{i}"], writes=[f"ss{i}"])
        P.stt("dve", xt[j2][:], xt[j2][:], ss[:, i:i + 1], fnb[:], ALU.mult, ALU.mult,
              reads=[f"xt{j2}", f"ss{i}", "fnb"], writes=[f"xt{j2}"])
        P.dma("act", ov[i], xt[j2][:], reads=[f"xt{j2}"])


def inputs_D(inp, resE, x1_cores):
    c = make_consts()
    w_out = np.ascontiguousarray(inp["w_out_odd"][0])
    fnb = np.ascontiguousarray(np.broadcast_to(inp["final_norm"][None, :], (128, D)).astype(np.float32))
    gb = np.ascontiguousarray(np.broadcast_to(np.stack([inp["q_gain"][0], inp["k_gain"][0]])[None], (128, 2, 128)).astype(np.float32))
    maps = []
    for core in range(NCORES):
        b = core // 4
        kTf = np.ascontiguousarray(np.concatenate([resE[b * 4 + q]["kT"] for q in range(4)], axis=2))
        vxf = np.ascontiguousarray(np.concatenate([resE[b * 4 + q]["vx"] for q in range(4)], axis=0))
        maps.append(dict(qT=resE[core]["qT"], kTf=kTf, vxf=vxf, szT=resE[core]["szT"], x1=x1_cores[core],
                         w_out=w_out, fnb=fnb, gb=gb, ident=c["ident"]))
    return maps


_NC_CACHE = {}


def _get_nc(mode):
    if mode not in _NC_CACHE:
        _NC_CACHE[mode] = build(mode)
    return _NC_CACHE[mode]


def _run(mode, maps):
    return run_bass_kernel_spmd(_get_nc(mode), maps, core_ids=list(range(NCORES))).results


def kernel_unfused(x, norm_even, w_in_even, conv_w, w_out_even, norm_odd, w_in_odd, q_gain, k_gain, w_out_odd, final_norm):
    inp = dict(x=np.asarray(x, np.float32), norm_even=np.asarray(norm_even, np.float32),
               w_in_even=np.asarray(w_in_even, np.float32), conv_w=np.asarray(conv_w, np.float32),
               w_out_even=np.asarray(w_out_even, np.float32), norm_odd=np.asarray(norm_odd, np.float32),
               w_in_odd=np.asarray(w_in_odd, np.float32), q_gain=np.asarray(q_gain, np.float32),
               k_gain=np.asarray(k_gain, np.float32), w_out_odd=np.asarray(w_out_odd, np.float32),
               final_norm=np.asarray(final_norm, np.float32))
    resA = _run("A", inputs_A(inp))
    u_full = np.stack([np.concatenate([resA[b * 4 + q]["u"] for q in range(4)], 0) for b in range(2)])
    resB = _run("B", inputs_B(u_full))
    ft_full = np.stack([resB[c]["ft"] for c in range(NCORES)])
    resC = _run("C", inputs_C(inp, ft_full, [r["yaT"] for r in resA], [r["sbzT"] for r in resA]))
    x1c = [resC[c]["x1"] for c in range(NCORES)]
    resE = _run("E", inputs_E(inp, x1c))
    resD = _run("D", inputs_D(inp, resE, x1c))
    out = np.stack([np.concatenate([resD[b * 4 + q]["out"] for q in range(4)], 0) for b in range(2)])
    return out.astype(np.float32)


GROUPS = [[0, 1, 2, 3], [4, 5, 6, 7]]


class _Stop(Exception):
    pass


def build_fused():
    try:
        return _build_fused()
    except _Stop as e:
        return e.args[0]


def _build_fused():
    stop_at = int(os.environ.get("FUSED_STOP", "99"))
    nphase = [0]
    nc = bass.Bass("TRN2", target_bir_lowering=False)
    IN, OUT = "ExternalInput", "ExternalOutput"
    dr = lambda n, s, d, k=IN: nc.dram_tensor(n, s, d, kind=k).ap()
    x = dr("x", [T, D], F32)
    xh = dr("xh", [2, D], F32)
    g0 = dr("g0", [128, 8], F32)
    w_in0 = dr("w_in0", [D, 3072], F32)
    cw = dr("cw", [128, 12], F32)
    identd = dr("ident", [128, 128], BF16)
    c1s1d = dr("c1s1", [128, 256], BF16)
    w2d = dr("w2", [128, 128 * 128], BF16)
    ccscd = dr("ccsc", [128, 256], BF16)
    w_out0 = dr("w_out0", [D, D], F32)
    g1 = dr("g1", [128, 8], F32)
    w_in1 = dr("w_in1", [D, 2560], F32)
    gv = dr("gv", [128, 4], F32)
    cosd = dr("cos", [128, T], F32)
    sind = dr("sin", [128, T], F32)
    p0d = dr("p0", [128, 128], BF16)
    onesd = dr("ones", [128, 128], BF16)
    w_out1 = dr("w_out1", [D, D], F32)
    fnbd = dr("fnb", [128, D], F32)
    gbd = dr("gb", [128, 2, 128], F32)
    out = dr("out", [T, D], F32, OUT)
    gin_u = nc.dram_tensor("gin_u", [4 * T, 128], BF16).ap()
    gout_u = nc.dram_tensor("gout_u", [16 * T, 128], BF16).ap()
    gin_f = nc.dram_tensor("gin_f", [4 * 128, T], BF16).ap()
    gout_f = nc.dram_tensor("gout_f", [16 * 128, T], BF16).ap()
    gin_k = nc.dram_tensor("gin_k", [2 * 128, T], BF16).ap()
    gout_k = nc.dram_tensor("gout_k", [8 * 128, T], BF16).ap()
    gin_v = nc.dram_tensor("gin_v", [T, 258], BF16).ap()
    gout_v = nc.dram_tensor("gout_v", [4 * T, 258], BF16).ap()

    with ExitStack() as st0:
        P = Prog(nc)
        ccsems = [st0.enter_context(nc.semaphore(f"cc{i}")) for i in range(11)]
        sbR = lambda stk, n, s, d: stk.enter_context(nc.sbuf_tensor("s_" + n, s, d, side="right"))
        mk_sb = lambda stk: (lambda n, s, d: stk.enter_context(nc.sbuf_tensor("s_" + n, s, d)))
        mk_ps = lambda stk: (lambda n, s, d=F32: stk.enter_context(nc.psum_tensor("p_" + n, s, d)))
        xres = sbR(st0, "xres", [128, NT, D], F32)
        ident = st0.enter_context(nc.sbuf_tensor("s_ident", [128, 128], BF16))
        ccscr = st0.enter_context(nc.sbuf_tensor("s_ccscr", [128, 64], F32))
        P.dma("sp", ident[:], identd, writes=["ident"])
        P.nosig_scratch = ccscr[:, 16:64]
        P.nosig_n = 0
        xv = x.rearrange("(i p) d -> i p d", p=128)

        def emit_phase():
            P.emit(st0)
            nphase[0] += 1
            if nphase[0] >= stop_at:
                raise _Stop(nc)

        def ag_start(i, src, dst, reads):
            sem = ccsems[i]
            P.op("pool", lambda e: e.collective_compute("AllGather", ALU.bypass, replica_groups=GROUPS,
                                                        ins=[src.opt()], outs=[dst.opt()]).then_inc(sem),
                 reads, [f"cc_inflight{i}"], nosignal=True)

        def ag_wait(i, writes):
            sem = ccsems[i]

            def fn(e):
                e.wait_ge(sem, 1)
                return e.memset(ccscr[:, i:i + 1], 0.0)
            P.op("pool", fn, [f"cc_inflight{i}"], writes)

        with ExitStack() as stL0:
            yaT = sbR(stL0, "yaT", [128, 4, T], BF16)
            sbzT = sbR(stL0, "sbzT", [128, 4, T], BF16)
            with ExitStack() as st:
                sb, ps = mk_sb(st), mk_ps(st)
                g0s = sb("g0s", [128, 8], F32)
                cws = sb("cws", [128, 12], F32)
                gfull = sb("gfull", [128, 8, 128], F32)
                hT = sb("hT", [128, 8, T], BF16)
                hTh = sb("hTh", [128, 8, 128], BF16)
                st1 = ExitStack()
                sb1 = mk_sb(st1)
                xht = sb1("xht", [128, D], F32)
                P.dma("sp", g0s[:], g0, writes=["g0s"])
                P.dma("sp", cws[:], cw, writes=["cws"])
                P.memset("pool", gfull[:], 1.0, writes=["gfull"])
                for k in range(8):
                    P.tsmul("pool", gfull[:, k, :], gfull[:, k, :], g0s[:, k:k + 1], reads=["g0s", "gfull"], writes=["gfull"])

                def loadA(i):
                    if i < NT:
                        P.dma("sp", xres[:, i, :], xv[i], writes=[f"xres{i}"])
                    else:
                        P.memset("pool", xht[:], 0.0, writes=["xht"])
                        P.dma("sp", xht[0:2, :], xh, writes=["xht"])

                pp = [ps(f"ppA{j}", [128, 512]) for j in range(4)]
                ppi = [0]
                wu_all = sb1("wu_all", [128, 8, 512], BF16)
                ustg = [sb1(f"ustg{j}", [128, 8, 128], F32) for j in range(2)]
                w0v = w_in0.rearrange("(k p) n -> p k n", p=128)
                for g_ in range(4):
                    P.dma("sp", ustg[g_ % 2][:], w0v[:, :, 2048 + g_ * 128:2048 + (g_ + 1) * 128], writes=[f"ustg{g_ % 2}"])
                    P.tt("pool", wu_all[:, :, g_ * 128:(g_ + 1) * 128], ustg[g_ % 2][:], gfull[:], ALU.mult,
                         reads=[f"ustg{g_ % 2}", "gfull"], writes=[f"wu{g_}"])
                ubt = [sb1(f"ubt{j}", [128, 512], BF16) for j in range(2)]
                ginu_t = gin_u.rearrange("(g i p) c -> i p g c", g=4, p=128)

                def u_tile(i):
                    if i >= NT:
                        return
                    pj = ppi[0] % 4
                    ppi[0] += 1
                    for k in range(8):
                        P.mm(pp[pj][:], hT[:, k, i * 128:(i + 1) * 128], wu_all[:, k, :], k == 0, k == 7,
                             reads=[f"wu{g_}" for g_ in range(4)] + [f"hT{i // 4}"], writes=[f"pp{pj}"])
                    P.copy("dve", ubt[i % 2][:], pp[pj][:], reads=[f"pp{pj}"], writes=[f"ubt{i % 2}"])
                    P.dma("sp", ginu_t[i], ubt[i % 2][:].rearrange("p (g c) -> p g c", g=4), reads=[f"ubt{i % 2}"], writes=[f"gin_u_t{i}"])

                for i in range(NT + 1):
                    loadA(i)
                rmsnorm_to_hT(P, nc, st1, lambda i: (xres[:, i, :] if i < NT else xht[:]),
                              lambda i: (f"xres{i}" if i < NT else "xht"), NT + 1,
                              lambda i: (hT[:, :, i * 128:(i + 1) * 128] if i < NT else hTh[:, :, :]),
                              lambda i: (f"hT{i // 4}" if i < NT else "hTh"), ident, "rnA", load_fn=None, after_tile=u_tile)
                emit_phase()
                st1.close()
                order = []
                for j in range(4):
                    order += [j, 8 + j, 4 + j, 12 + j]
                order += [20, 21, 22, 23]
                ph = ps("phA", [128, 2, 2])
                tbuf = sb("tbuf", [128, T + 2], F32)
                cbuf = sb("cbuf", [128, T], F32)
                axs = [sb(f"axs{j}", [128, 512], F32) for j in range(2)]
                szs = [sb(f"szs{j}", [128, 512], F32) for j in range(2)]
                vs = [sb(f"vs{j}", [128, 512], F32) for j in range(2)]
                hprod = sb("hprod", [128, 2, 2], F32)

                def projA(wb, wtok, tb):
                    j = ppi[0] % 4
                    ppi[0] += 1
                    for k in range(8):
                        P.mm(pp[j][:], wb[:, k, :], hT[:, k, tb * 512:(tb + 1) * 512], k == 0, k == 7,
                             reads=[wtok, f"hT{tb}"], writes=[f"pp{j}"])
                    return pp[j], f"pp{j}"

                chunks = {}
                NBA = 16
                genA = stream_w_chunks(P, nc, sb, w_in0, gfull, 3072, order, "wiA", nstage=2, nbuf=NBA, engs=("dve",), eager=True, ahead=NBA - 2)
                firstA = next(genA)
                for g_ in range(4):
                    ag_start(g_, gin_u[g_ * T:(g_ + 1) * T, :], gout_u[g_ * 4 * T:(g_ + 1) * 4 * T, :],
                             [f"gin_u_t{i}" for i in range(NT)] + [f"wiA_wb{jj}" for jj in range(NBA - 1)])

                def _chainA():
                    yield firstA
                    for it in genA:
                        yield it
                for cid, wb, wtok in _chainA():
                    kind, j = cid // 4, cid % 4
                    if kind == 4:
                        for i in range(NT):
                            pj = ppi[0] % 4
                            ppi[0] += 1
                            for k in range(8):
                                P.mm(pp[pj][:, 0:128], hT[:, k, i * 128:(i + 1) * 128], wb[:, k, :], k == 0, k == 7,
                                     reads=[wtok, f"hT{i // 4}"], writes=[f"pp{pj}"])
                            P.copy("act", ub[j % 2][:, i, :], pp[pj][:, 0:128], reads=[f"pp{pj}"], writes=[f"ub{j % 2}"])
                        P.dma("act", ginu_v[j], ub[j % 2][:], reads=[f"ub{j % 2}"], writes=[f"gin_u{j}"])
                        ag_start(j, gin_u[j * T:(j + 1) * T, :], gout_u[j * 4 * T:(j + 1) * 4 * T, :], [f"gin_u{j}"])
                    elif kind == 5:
                        for tb in range(4):
                            pt, ptok = projA(wb, wtok, tb)
                            P.act(sbzT[:, j, tb * 512:(tb + 1) * 512], pt[:], AF.Silu, reads=[ptok], writes=[f"sbzT{j}"])
                    elif kind == 0:
                        chunks["x"] = (wb, wtok)
                    elif kind == 2:
                        wbx, wtokx = chunks["x"]
                        for tb in range(4):
                            ptx, ptokx = projA(wbx, wtokx, tb)
                            a = tb % 2
                            P.copy("act", axs[a][:], ptx[:], reads=[ptokx], writes=[f"axs{a}"])
                            ptc, ptokc = projA(wb, wtok, tb)
                            P.tt("dve", tbuf[:, 1 + tb * 512:1 + (tb + 1) * 512], ptc[:], axs[a][:], ALU.mult,
                                 reads=[ptokc, f"axs{a}"], writes=[f"tbuf{tb}"])
                        for k in range(8):
                            P.mm(ph[:, 0, :], wbx[:, k, :], hTh[:, k, 0:2], k == 0, k == 7, reads=[wtokx, "hTh"], writes=["phb"])
                        for k in range(8):
                            P.mm(ph[:, 1, :], wb[:, k, :], hTh[:, k, 0:2], k == 0, k == 7, reads=[wtok, "hTh"], writes=["phb"])
                        P.copy("act", hprod[:, 0, :], ph[:, 0, :], reads=["phb"], writes=["hprod0"])
                        P.tt("dve", hprod[:, 1, :], ph[:, 1, :], hprod[:, 0, :], ALU.mult, reads=["phb", "hprod0"], writes=["hprod1"])
                        P.copy("dve", tbuf[:, 0:1], hprod[:, 1, 0:1], reads=["hprod1"], writes=["tbufL"])
                        P.copy("dve", tbuf[:, T + 1:T + 2], hprod[:, 1, 1:2], reads=["hprod1"], writes=["tbufR"])
                        alltb = [f"tbuf{tb}" for tb in range(4)]
                        P.act(cbuf[:], tbuf[:, 1:T + 1], AF.Copy, reads=alltb, writes=["cbuf"], scale=cws[:, 3 * j + 1:3 * j + 2])
                        P.stt("dve", cbuf[:], tbuf[:, 0:T], cws[:, 3 * j:3 * j + 1], cbuf[:], ALU.mult, ALU.add,
                              reads=alltb + ["tbufL", "cws"], writes=["cbuf"])
                        P.stt("dve", cbuf[:], tbuf[:, 2:T + 2], cws[:, 3 * j + 2:3 * j + 3], cbuf[:], ALU.mult, ALU.add,
                              reads=alltb + ["tbufR", "cws"], writes=["cbuf"])
                    elif kind == 1:
                        chunks["b"] = (wb, wtok)
                    elif kind == 3:
                        wbb, wtokb = chunks["b"]
                        for tb in range(4):
                            a = tb % 2
                            ptz, ptokz = projA(wb, wtok, tb)
                            P.act(szs[a][:], ptz[:], AF.Silu, reads=[ptokz], writes=[f"szs{a}"])
                            ptb, ptokb = projA(wbb, wtokb, tb)
                            P.tt("dve", vs[a][:], ptb[:], szs[a][:], ALU.mult, reads=[ptokb, f"szs{a}"], writes=[f"vs{a}"])
                            P.tt("dve", yaT[:, j, tb * 512:(tb + 1) * 512], cbuf[:, tb * 512:(tb + 1) * 512], vs[a][:], ALU.mult,
                                 reads=["cbuf", f"vs{a}"], writes=[f"yaT{j}"])
                for g_ in range(4):
                    ag_wait(g_, [f"gout_u{g_}"])
                emit_phase()
            with ExitStack() as stB:
                X1 = mk_sb(stB)("X1", [128, 128, 128], BF16)
                WoA = mk_sb(stB)("WoA", [128, 4, D], BF16)
                with ExitStack() as st:
                    sb, ps = mk_sb(st), mk_ps(st)
                    X0 = sb("X0", [128, 2, 64, 128], BF16)
                    c1s1 = sb("c1s1", [128, 256], BF16)
                    pb = [ps(f"pbA{j}", [128, 512]) for j in range(4)]
                    P.dma("sp", c1s1[:], c1s1d, writes=["c1s1"])
                    def x0_load(r, q):
                        def fn(e):
                            rank = P.rank4(e)
                            src = gout_u.rearrange("(g q t) c -> g q t c", q=4, g=4)[bass.ds(rank, 1)]
                            src = src.rearrange("o q (p j) c -> (o q) p j c", j=64)[q]
                            return e.dma_start(out=X0[q * 32:(q + 1) * 32, r, :, :], in_=src)
                        return fn
                    for q in range(4):
                        P.op("pool", x0_load(0, q), [f"gout_u{g_}" for g_ in range(4)], [f"X0a{q}"], dma=True,
                             own_sem=st0.enter_context(nc.semaphore(f"dyn_x0a{q}")))
                        P.op("pool", x0_load(1, q), [f"gout_u{g_}" for g_ in range(4)], [f"X0b{q}"], dma=True,
                             own_sem=st0.enter_context(nc.semaphore(f"dyn_x0b{q}")))
                    wstA = [sb(f"wstA{j}", [128, 512], F32) for j in range(3)]
                    nwA = 0
                    for nb in range(2):
                        for kc in range(4):
                            P.dma("sp", wstA[nwA % 3][:], w_out0[kc * 128:(kc + 1) * 128, nb * 512:(nb + 1) * 512], reads=[f"X0a{q}" for q in range(4)] + [f"X0b{q}" for q in range(4)], writes=[f"wstA{nwA % 3}"])
                            P.copy("pool", WoA[:, kc, nb * 512:(nb + 1) * 512], wstA[nwA % 3][:],
                                   reads=[f"wstA{nwA % 3}"], writes=[f"Wo0_{kc}_{nb}"])
                            nwA += 1

                    n = 0
                    for c0 in range(0, 128, 2):
                        j = n % 4
                        n += 1
                        pt = pb[j][:].rearrange("p (a b) -> p a b", b=256)
                        for cc in range(2):
                            P.mm(pt[:, cc, :], X0[:, :, :, c0 + cc], c1s1[:], True, True, reads=[f"X0a{q}" for q in range(4)] + [f"X0b{q}" for q in range(4)] + ["c1s1"], writes=[f"pb{j}"])
                        e1, e2 = ("dve", "act") if (c0 // 2) % 2 == 0 else ("act", "dve")
                        P.copy(e1, X1[0:64, c0:c0 + 2, :], pt[0:64, :, 0:128], reads=[f"pb{j}"], writes=[f"X1r{c0}", f"lkb{j}"])
                        P.copy(e1, X1[64:128, c0:c0 + 2, :], pt[64:128, :, 128:256], reads=[f"pb{j}", f"lkb{j}"], writes=[f"X1i{c0}"])
                    emit_phase()
                with ExitStack() as st:
                    sb, ps = mk_sb(st), mk_ps(st)
                    W2 = sb("W2", [128, 128, 128], BF16)
                    X2 = sb("X2", [128, 2, S], BF16)
                    ccsc = sb("ccsc", [128, 256], BF16)
                    fo = [sb(f"fo{j}", [128, 1024], BF16) for j in range(2)]
                    pb = [ps(f"pbB{j}", [128, 512]) for j in range(4)]
                    X2r, X2i = X2[:, 0, :], X2[:, 1, :]
                    P.dma("sp", ccsc[:], ccscd, writes=["ccsc"])
                    w2v = w2d.rearrange("p (a b) -> p a b", b=128)
                    for h in range(4):
                        P.dma("sp", W2[:, h * 32:(h + 1) * 32, :], w2v[:, h * 32:(h + 1) * 32, :], writes=[f"W2_{h}"])
                    n = 0
                    for k0 in range(0, 128, 4):
                        j = n % 4
                        n += 1
                        pt = pb[j][:].rearrange("p (a b) -> p a b", b=128)
                        for kk in range(4):
                            k1 = k0 + kk
                            P.mm(pt[:, kk, :], X1[:, :, k1], W2[:, k1, :], True, True, reads=[f"W2_{k1 // 32}"], writes=[f"pb{j}"])
                        X2v = X2[:].rearrange("p r (k1 k2) -> p k1 r k2", k2=64)[:, k0:k0 + 4, :, :]
                        P.copy("dve" if (k0 // 4) % 2 == 0 else "act", X2v, pb[j][:].rearrange("p (a r b) -> p a r b", r=2, b=64),
                               reads=[f"pb{j}"], writes=[f"X2_{k0}"])
                    x2all = [f"X2_{k0}" for k0 in range(0, 128, 4)]
                    ginf_v = gin_f.rearrange("(q l) t -> q l t", q=4)
                    for kb in range(16):
                        j = n % 4
                        n += 1
                        rr_ = X2r.rearrange("p (k1 k2) -> p k2 k1", k2=64)[:, 4 * kb:4 * kb + 4, :]
                        ri_ = X2i.rearrange("p (k1 k2) -> p k2 k1", k2=64)[:, 4 * kb:4 * kb + 4, :]
                        P.mm(pb[j][:], ccsc[:, 0:128], rr_, True, False, reads=x2all + ["ccsc"], writes=[f"pb{j}"])
                        P.mm(pb[j][:], ccsc[:, 128:256], ri_, False, True, reads=x2all + ["ccsc"], writes=[f"pb{j}"])
                        f = kb // 2
                        P.copy("dve" if kb % 2 == 0 else "act", fo[f % 2][:, (kb % 2) * 512:(kb % 2 + 1) * 512], pb[j][:],
                               reads=[f"pb{j}"], writes=[f"fo{f % 2}_{kb % 2}"])
                        if kb % 2 == 1:
                            P.dma("sp", ginf_v[f // 2][:, (f % 2) * 1024:(f % 2 + 1) * 1024], fo[f % 2][:],
                                  reads=[f"fo{f % 2}_0", f"fo{f % 2}_1"], writes=[f"gin_f{f}"])
                            if f % 2 == 1:
                                q_ = f // 2
                                ag_start(4 + q_, gin_f[q_ * 128:(q_ + 1) * 128, :], gout_f[q_ * 512:(q_ + 1) * 512, :],
                                         [f"gin_f{2 * q_}", f"gin_f{2 * q_ + 1}"])
                    emit_phase()
                with ExitStack() as st:
                    sb, ps = mk_sb(st), mk_ps(st)
                    ybT = sb("ybT", [128, 4, T], BF16)
                    fst = [sb(f"fst{j}", [128, T], BF16) for j in range(2)]
                    WoB = sb("WoB", [128, 4, D], BF16)
                    wst = [sb(f"wstC{j}", [128, 512], F32) for j in range(3)]
                    pso = [ps(f"psoC{j}", [128, 512]) for j in range(4)]

                    def ft_load(g, dst):
                        def fn(e):
                            rank = P.rank4(e)
                            src = gout_f.rearrange("(q g l) t -> q g l t", g=4, q=4)[bass.ds(rank, 1)]
                            src = src.rearrange("o g l t -> (o g) l t")[g]
                            return e.dma_start(out=dst, in_=src)
                        return fn
                    n = 0
                    for nb in range(2):
                        for i in range(NT):
                            j = n % 4
                            n += 1
                            for kc in range(4):
                                P.mm(pso[j][:], yaT[:, kc, i * 128:(i + 1) * 128], WoA[:, kc, nb * 512:(nb + 1) * 512], kc == 0, kc == 3,
                                     reads=[f"yaT{kc}", f"Wo0_{kc}_{nb}"], writes=[f"pso{j}"])
                            P.tt("dve", xres[:, i, nb * 512:(nb + 1) * 512], pso[j][:], xres[:, i, nb * 512:(nb + 1) * 512], ALU.add,
                                 reads=[f"pso{j}", f"xres{i}"], writes=[f"xres{i}"])
                    nw = 0
                    for nb in range(2):
                        for kc in range(4):
                            P.dma("sp", wst[nw % 3][:], w_out0[(4 + kc) * 128:(5 + kc) * 128, nb * 512:(nb + 1) * 512], writes=[f"wst{nw % 3}"])
                            P.copy("act", WoB[:, kc, nb * 512:(nb + 1) * 512], wst[nw % 3][:],
                                   reads=[f"wst{nw % 3}"], writes=[f"Wo0_{4 + kc}_{nb}"])
                            nw += 1
                    for q_ in range(4):
                        ag_wait(4 + q_, [f"gout_f{q_}"])
                    for g in range(4):
                        P.op("pool", ft_load(g, fst[g % 2][:]), [f"gout_f{q_}" for q_ in range(4)], [f"fst{g % 2}"], dma=True,
                             own_sem=st0.enter_context(nc.semaphore(f"dyn_ft{g}")))
                        P.tt("dve", ybT[:, g, :], fst[g % 2][:], sbzT[:, g, :], ALU.mult,
                             reads=[f"fst{g % 2}", f"sbzT{g}"], writes=[f"ybT{g}"])
                    for nb in range(2):
                        for i in range(NT):
                            j = n % 4
                            n += 1
                            for kc in range(4):
                                P.mm(pso[j][:], ybT[:, kc, i * 128:(i + 1) * 128], WoB[:, kc, nb * 512:(nb + 1) * 512], kc == 0, kc == 3,
                                     reads=[f"ybT{kc}", f"Wo0_{4 + kc}_{nb}"], writes=[f"pso{j}"])
                            P.tt("dve", xres[:, i, nb * 512:(nb + 1) * 512], pso[j][:], xres[:, i, nb * 512:(nb + 1) * 512], ALU.add,
                                 reads=[f"pso{j}", f"xres{i}"], writes=[f"xres{i}"])
                    emit_phase()
        with ExitStack() as stL1:
            qT = sbR(stL1, "qT", [128, 8, T], BF16)
            with ExitStack() as stE:
                sbE = mk_sb(stE)
                hT = sbE("hT1", [128, 8, T], BF16)
                g1s = sbE("g1s", [128, 8], F32)
                gfull = sbE("gfull1", [128, 8, 128], F32)
                with ExitStack() as st:
                    sb, ps = mk_sb(st), mk_ps(st)
                    p0 = sb("p0", [128, 128], BF16)
                    ones = sb("ones", [128, 128], BF16)
                    gvs = sb("gvs", [128, 4], F32)
                    cos = sb("cos", [128, T], F32)
                    sin = sb("sin", [128, T], F32)
                    for t_, d_, n_ in ((p0, p0d, "p0"), (ones, onesd, "ones"), (g1s, g1, "g1s"), (gvs, gv, "gvs"),
                                       (cos, cosd, "cos"), (sin, sind, "sin")):
                        P.dma("sp", t_[:], d_, writes=[n_])
                    P.memset("pool", gfull[:], 1.0, writes=["gfull"])
                    for k in range(8):
                        P.tsmul("pool", gfull[:, k, :], gfull[:, k, :], g1s[:, k:k + 1], reads=["g1s", "gfull"], writes=["gfull"])
                    stR = ExitStack()
                    rmsnorm_to_hT(P, nc, stR, lambda i: xres[:, i, :], lambda i: f"xres{i}", NT,
                                  lambda i: hT[:, :, i * 128:(i + 1) * 128], lambda i: f"hT{i // 4}", ident, "rnE")
                    emit_phase()
                    stR.close()
                    pp = [ps(f"ppE{j}", [128, 512]) for j in range(4)]
                    pss = ps("pss", [128, 512])
                    prot = ps("prot", [128, 512])
                    qb = [sb(f"qb{j}", [128, 512], BF16) for j in range(3)]
                    sq = [sb(f"sq{j}", [128, 512], BF16) for j in range(3)]
                    rs = [sb(f"rs{j}", [128, 512], F32) for j in range(2)]
                    ta = [sb(f"ta{j}", [128, 512], F32) for j in range(2)]
                    tb_ = [sb(f"tb{j}", [128, 512], F32) for j in range(2)]
                    kbuf = sb("kbuf", [128, 2, T], BF16)
                    vbuf = sb("vbuf", [128, NT, 129], BF16)
                    ppi = [0]
                    cnt = [0]

                    def projE(wb, wtok, tb):
                        j = ppi[0] % 4
                        ppi[0] += 1
                        for k in range(8):
                            P.mm(pp[j][:], wb[:, k, :], hT[:, k, tb * 512:(tb + 1) * 512], k == 0, k == 7,
                                 reads=[wtok, f"hT{tb}"], writes=[f"pp{j}"])
                        return pp[j], f"pp{j}"

                    gink_v = gin_k.rearrange("(g d) t -> g d t", g=2)
                    ginv_v = gin_v.rearrange("(i p) (a b) -> a p i b", p=128, a=2)
                    epsT = sb("epsT", [128, 1], F32)
                    P.memset("dve", epsT[:], EPS, writes=["epsT"])
                    order = [8, 9, 10, 11] + list(range(8))
                    pend = []

                    def stage1(cid, wb, wtok, tb):
                        a = cnt[0] % 3
                        cnt[0] += 1
                        pt, ptok = projE(wb, wtok, tb)
                        P.copy("act", qb[a][:], pt[:], reads=[ptok], writes=[f"qb{a}", f"lk1_{ptok}"])
                        P.act(sq[a][:], pt[:], AF.Square, reads=[ptok], writes=[f"sq{a}", f"lk2_{ptok}"])
                        return (cid, tb, a, pt, ptok)

                    s2cnt = [0]

                    def stage2(item):
                        cid, tb, a, pt, ptok = item
                        b = s2cnt[0] % 2
                        s2cnt[0] += 1
                        isq = cid < 8
                        gcol = 0 if isq else 2
                        sl = slice(tb * 512, (tb + 1) * 512)
                        P.mm(pss[:], ones[:], sq[a][:], True, True, reads=["ones", f"sq{a}"], writes=["pss"])
                        P.mm(prot[:], p0[:], qb[a][:], True, True, reads=["p0", f"qb{a}"], writes=["prot"])
                        P.act(rs[b][:], pss[:], AF.Ln, reads=["pss", "epsT"], writes=[f"rs{b}"], scale=1.0 / 128, bias=epsT[:, 0:1])
                        P.act(rs[b][:], rs[b][:], AF.Exp, reads=[f"rs{b}"], writes=[f"rs{b}"], scale=-0.5)
                        P.stt("dve", ta[b][:], pt[:], gvs[:, gcol:gcol + 1], cos[:, sl], ALU.mult, ALU.mult,
                              reads=[ptok, f"lk1_{ptok}", f"lk2_{ptok}", "gvs", "cos"], writes=[f"ta{b}"])
                        P.stt("dve", tb_[b][:], prot[:], gvs[:, gcol + 1:gcol + 2], sin[:, sl], ALU.mult, ALU.mult,
                              reads=["prot", "gvs", "sin"], writes=[f"tb{b}"])
                        P.tt("dve", ta[b][:], ta[b][:], tb_[b][:], ALU.add, reads=[f"ta{b}", f"tb{b}"], writes=[f"ta{b}"])
                        if isq:
                            P.tt("dve", qT[:, cid, sl], ta[b][:], rs[b][:], ALU.mult, reads=[f"ta{b}", f"rs{b}"], writes=[f"q{cid}_{tb}"])
                        else:
                            P.tt("dve", kbuf[:, cid - 8, sl], ta[b][:], rs[b][:], ALU.mult, reads=[f"ta{b}", f"rs{b}"], writes=[f"kbuf{cid - 8}"])
                            if tb == 3:
                                P.dma("sp", gink_v[cid - 8], kbuf[:, cid - 8, :], reads=[f"kbuf{cid - 8}"], writes=[f"gin_k{cid - 8}"])
                                if cid == 9:
                                    ag_start(8, gin_k, gout_k, ["gin_k0", "gin_k1"])

                    for cid, wb, wtok in stream_w_chunks(P, nc, sb, w_in1, gfull, 2560, order, "wiE", nstage=2, nbuf=10, engs=("dve",), eager=True):
                        if cid < 10:
                            for tb in range(4):
                                item = stage1(cid, wb, wtok, tb)
                                if len(pend) >= 2:
                                    stage2(pend.pop(0))
                                pend.append(item)
                        else:
                            while pend:
                                stage2(pend.pop(0))
                            kvh = cid - 10
                            P.memset("pool", vbuf[:, :, 128:129], 1.0, writes=["vbuf"])
                            for i in range(NT):
                                j = ppi[0] % 4
                                ppi[0] += 1
                                for k in range(8):
                                    P.mm(pp[j][:, 0:128], hT[:, k, i * 128:(i + 1) * 128], wb[:, k, :], k == 0, k == 7,
                                         reads=[wtok, f"hT{i // 4}"], writes=[f"pp{j}"])
                                P.copy("act", vbuf[:, i, 0:128], pp[j][:, 0:128], reads=[f"pp{j}", "vbuf"], writes=[f"vbuf_{i}"])
                            P.dma("sp", ginv_v[kvh], vbuf[:], reads=[f"vbuf_{i}" for i in range(NT)] + ["vbuf"], writes=["vbuf", f"gin_v{kvh}"])
                            if kvh == 1:
                                for h_ in range(2):
                                    ag_start(9 + h_, gin_v[h_ * 1024:(h_ + 1) * 1024, :], gout_v[h_ * 4096:(h_ + 1) * 4096, :],
                                             ["gin_v0", "gin_v1"])
                    while pend:
                        stage2(pend.pop(0))
                    ag_wait(8, ["gout_k"])
                    for h_ in range(2):
                        ag_wait(9 + h_, [f"gout_v{h_}"])
                    emit_phase()
                szT = sbR(stL1, "szT", [128, 8, T], BF16)
                with ExitStack() as st:
                    sb, ps = mk_sb(st), mk_ps(st)
                    pp = [ps(f"ppZ{j}", [128, 512]) for j in range(4)]
                    n = 0
                    for cid, wb, wtok in stream_w_chunks(P, nc, sb, w_in1, gfull, 2560, list(range(12, 20)), "wiZ"):
                        for tb in range(4):
                            j = n % 4
                            n += 1
                            for k in range(8):
                                P.mm(pp[j][:], wb[:, k, :], hT[:, k, tb * 512:(tb + 1) * 512], k == 0, k == 7,
                                     reads=[wtok, f"hT{tb}"], writes=[f"pp{j}"])
                            P.act(szT[:, cid - 12, tb * 512:(tb + 1) * 512], pp[j][:], AF.Silu, reads=[f"pp{j}"], writes=[f"sz{cid - 12}"])
                    emit_phase()
            with ExitStack() as st:
                sb, ps = mk_sb(st), mk_ps(st)
                gb = sb("gb", [128, 2, 128], F32)
                mx = sb("mx", [128, 4], F32)
                kT = sb("kT", [128, 2, S], BF16)
                V = sb("V", [128, 64, 258], BF16)
                PT = [sb(f"PT{j}", [128, 2, 512], BF16) for j in range(4)]
                on = [sb(f"on{j}", [128, 128], BF16) for j in range(4)]
                rden = sb("rden", [128, 8], F32)
                oc = [sb(f"oc{j}", [128, 2, 129], F32) for j in range(2)]
                pS = [ps(f"pS{j}", [128, 2, 512]) for j in range(3)]
                pO = [ps(f"pO{j}", [128, 2, 129]) for j in range(2)]
                slot = [0]
                sslot = {}
                P.dma("sp", gb[:], gbd, writes=["gb"])
                gk_v = gout_k.rearrange("(q g d) t -> g d q t", q=4, g=2)
                Vd = V[:].rearrange("p (q h c) f -> p q h c f", q=4, h=2)
                vsrc = [gout_v[h_ * 4096:(h_ + 1) * 4096, :].rearrange("(q c p) f -> p q c f", q=4, p=128) for h_ in range(2)]
                for q_ in range(4):
                    P.dma("sp", kT[:, 0, q_ * T:(q_ + 1) * T], gk_v[0][:, q_, :], reads=["gout_k"], writes=[f"kT0_{q_}"])
                    for h_ in range(2):
                        P.dma("sp", Vd[:, q_, h_, :, :], vsrc[h_][:, q_, :, :], reads=[f"gout_v{h_}"], writes=[f"V{q_}_{h_}"])
                for q_ in range(4):
                    P.dma("sp", kT[:, 1, q_ * T:(q_ + 1) * T], gk_v[1][:, q_, :], reads=["gout_k"], writes=[f"kT1_{q_}"])
                scale = 128.0 ** -0.5
                P.op("dve", lambda e: e.reduce_max(mx[:, 0:1], gb[:, 0, :], AX.X, apply_absolute_value=True), reads=["gb"], writes=["mx0"])
                P.op("dve", lambda e: e.reduce_max(mx[:, 1:2], gb[:, 1, :], AX.X, apply_absolute_value=True), reads=["gb"], writes=["mx1"])
                P.tt("dve", mx[:, 2:3], mx[:, 0:1], mx[:, 1:2], ALU.mult, reads=["mx0", "mx1"], writes=["mx2"])
                P.ts("dve", mx[:, 3:4], mx[:, 2:3], -scale * 128.0, 0.0, ALU.mult, ALU.add, reads=["mx2"], writes=["nbias"])
                Vv = V[:].rearrange("p c (g f) -> p c g f", g=2)
                iters = [(h, qb_, kc2) for h in range(8) for qb_ in range(4) for kc2 in range(32)]
                NI = len(iters)

                def emit_S(n):
                    h, qb_, kc2 = iters[n]
                    g = h // 4
                    j = slot[0] % 3
                    slot[0] += 1
                    jp = n % 4
                    qsl = slice(qb_ * 512, (qb_ + 1) * 512)
                    for c2 in range(2):
                        kc = 2 * kc2 + c2
                        P.mm(pS[j][:, c2, :], kT[:, g, kc * 128:(kc + 1) * 128], qT[:, h, qsl], True, True,
                             reads=[f"kT{g}_{kc // 16}", f"q{h}_{qb_}"], writes=[f"pS{j}"])
                    P.act(PT[jp][:], pS[j][:], AF.Exp, reads=[f"pS{j}", "nbias"], writes=[f"PT{jp}"], scale=scale, bias=mx[:, 3:4])

                def emit_PV(n):
                    h, qb_, kc2 = iters[n]
                    g = h // 4
                    jp = n % 4
                    qsl = slice(qb_ * 512, (qb_ + 1) * 512)
                    for c2 in range(2):
                        kc = 2 * kc2 + c2
                        for qt in range(4):
                            P.mm(pO[qt // 2][:, qt % 2, :], PT[jp][:, c2, qt * 128:(qt + 1) * 128], Vv[:, kc, g, :],
                                 kc == 0, kc == 63, reads=[f"PT{jp}", f"V{kc // 16}_{(kc % 16) // 8}"], writes=[f"pOb{qt // 2}"])
                    if kc2 == 31:
                        for b_ in range(2):
                            P.copy("dve", oc[b_][:], pO[b_][:], reads=[f"pOb{b_}"], writes=[f"oc{b_}"])
                        for qt in range(4):
                            acc = oc[qt // 2][:, qt % 2, :]
                            col = (h * 4 + qb_) % 2 * 4 + qt
                            P.recip(rden[:, col:col + 1], acc[:, 128:129], reads=[f"oc{qt // 2}"], writes=[f"rden{col}"])
                            P.tsmul("dve", on[qt][:], acc[:, 0:128], rden[:, col:col + 1], reads=[f"oc{qt // 2}", f"rden{col}"], writes=[f"on{qt}"])
                        deferred.append((h, qb_))

                def emit_fin(h, qb_):
                    qsl = slice(qb_ * 512, (qb_ + 1) * 512)
                    j = slot[0] % 3
                    slot[0] += 1
                    pTv = pS[j][:, 0, 0:256].bitcast(BF16)
                    for qt in range(4):
                        P.tr(pTv[:, qt * 128:(qt + 1) * 128], on[qt][:], ident[:], reads=[f"on{qt}", "ident"], writes=[f"pS{j}"])
                    P.tt("dve", qT[:, h, qsl], pTv, szT[:, h, qsl], ALU.mult,
                         reads=[f"pS{j}", f"sz{h}"], writes=[f"q{h}_{qb_}"])

                deferred = []
                LOOK = 2
                for n in range(NI + LOOK):
                    if n < NI:
                        emit_S(n)
                    if n >= LOOK:
                        m = n - LOOK
                        emit_PV(m)
                        if len(deferred) and iters[m][2] == 0:
                            had = list(deferred)
                            del deferred[:]
                            for (h_, q__) in had:
                                emit_fin(h_, q__)
                for (h_, q__) in deferred:
                    emit_fin(h_, q__)
                emit_phase()
            with ExitStack() as st:
                sb, ps = mk_sb(st), mk_ps(st)
                Wo = sb("Wo1", [128, 8, D], BF16)
                wst = [sb(f"wstD{j}", [128, 512], F32) for j in range(3)]
                fnb = sb("fnb", [128, D], F32)
                junk = sb("junkD", [128, D], BF16)
                ss = sb("ssD", [128, NT], F32)
                epsD = sb("epsD", [128, 1], F32)
                pso = [ps(f"psoD{j}", [128, 512]) for j in range(4)]
                nw = 0
                for nb in range(2):
                    for kc in range(8):
                        P.dma("sp", wst[nw % 3][:], w_out1[kc * 128:(kc + 1) * 128, nb * 512:(nb + 1) * 512], writes=[f"wst{nw % 3}"])
                        P.copy(("dve", "pool", "act")[nw % 3], Wo[:, kc, nb * 512:(nb + 1) * 512], wst[nw % 3][:],
                               reads=[f"wst{nw % 3}"], writes=[f"Wo1_{kc}_{nb}"])
                        nw += 1
                P.dma("sp", fnb[:], fnbd, writes=["fnb"])
                ov = out.rearrange("(i p) d -> i p d", p=128)
                P.memset("dve", ss[:], 0.0, writes=["ss"])
                P.memset("dve", epsD[:], EPS, writes=["epsD"])
                n = 0
                for nb in range(2):
                    for i in range(NT):
                        j = n % 4
                        n += 1
                        for h in range(8):
                            P.mm(pso[j][:], qT[:, h, i * 128:(i + 1) * 128], Wo[:, h, nb * 512:(nb + 1) * 512], h == 0, h == 7,
                                 reads=[f"q{h}_{i // 4}", f"Wo1_{h}_{nb}"], writes=[f"pso{j}"])
                        P.tt("dve", xres[:, i, nb * 512:(nb + 1) * 512], pso[j][:], xres[:, i, nb * 512:(nb + 1) * 512], ALU.add,
                             reads=[f"pso{j}", f"xres{i}"], writes=[f"xres{i}"])
                        if nb == 1:
                            P.act(junk[:], xres[:, i, :], AF.Square, reads=[f"xres{i}", "ss"], writes=["junk", f"ss{i}"], accum_out=ss[:, i:i + 1])
                            P.act(ss[:, i:i + 1], ss[:, i:i + 1], AF.Ln, reads=[f"ss{i}", "epsD"], writes=[f"ss{i}"], scale=1.0 / D, bias=epsD[:, 0:1])
                            P.act(ss[:, i:i + 1], ss[:, i:i + 1], AF.Exp, reads=[f"ss{i}"], writes=[f"ss{i}"], scale=-0.5)
                            P.stt("dve", xres[:, i, :], xres[:, i, :], ss[:, i:i + 1], fnb[:], ALU.mult, ALU.mult,
                                  reads=[f"xres{i}", f"ss{i}", "fnb"], writes=[f"xres{i}"])
                            P.dma("sp", ov[i], xres[:, i, :], reads=[f"xres{i}"])
                emit_phase()
    return nc


def inputs_fused(inp):
    c = make_consts()
    w2 = fft_consts_packed(c)
    x = inp["x"]
    g0 = np.ascontiguousarray(inp["norm_even"][0].reshape(8, 128).T)
    g1 = np.ascontiguousarray(inp["norm_odd"][0].reshape(8, 128).T)
    cw = np.ascontiguousarray(inp["conv_w"][0].reshape(3, 4, 128).transpose(2, 1, 0).reshape(128, 12))
    gq, gk = inp["q_gain"][0], inp["k_gain"][0]
    partner = np.arange(128) ^ 1
    gv = np.ascontiguousarray(np.stack([gq, gq[partner], gk, gk[partner]], axis=1).astype(np.float32))
    fnb = np.ascontiguousarray(np.broadcast_to(inp["final_norm"][None, :], (128, D)).astype(np.float32))
    gb = np.ascontiguousarray(np.broadcast_to(np.stack([gq, gk])[None], (128, 2, 128)).astype(np.float32))
    shared = dict(g0=g0, w_in0=np.ascontiguousarray(inp["w_in_even"][0]), cw=cw, ident=c["ident"], c1s1=c["c1s1"], w2=w2,
                  ccsc=c["ccsc"], w_out0=np.ascontiguousarray(inp["w_out_even"][0]), g1=g1,
                  w_in1=np.ascontiguousarray(inp["w_in_odd"][0]), gv=gv, p0=c["p0"], ones=c["ones"],
                  w_out1=np.ascontiguousarray(inp["w_out_odd"][0]), fnb=fnb, gb=gb)
    maps = []
    for core in range(NCORES):
        b, q = core // 4, core % 4
        cosT, sinT = rope_tables(core)
        m = dict(shared)
        m.update(x=np.ascontiguousarray(x[b, q * T:(q + 1) * T]), xh=halo_rows(x, core), cos=cosT, sin=sinT)
        maps.append(m)
    return maps


def kernel_fused(inp):
    if "F" not in _NC_CACHE:
        _NC_CACHE["F"] = build_fused()
    res = run_bass_kernel_spmd(_NC_CACHE["F"], inputs_fused(inp), core_ids=list(range(NCORES))).results
    out = np.stack([np.concatenate([res[b * 4 + q]["out"] for q in range(4)], 0) for b in range(2)])
    return out.astype(np.float32)


def kernel(x, norm_even, w_in_even, conv_w, w_out_even, norm_odd, w_in_odd, q_gain, k_gain, w_out_odd, final_norm):
    inp = dict(x=np.asarray(x, np.float32), norm_even=np.asarray(norm_even, np.float32),
               w_in_even=np.asarray(w_in_even, np.float32), conv_w=np.asarray(conv_w, np.float32),
               w_out_even=np.asarray(w_out_even, np.float32), norm_odd=np.asarray(norm_odd, np.float32),
               w_in_odd=np.asarray(w_in_odd, np.float32), q_gain=np.asarray(q_gain, np.float32),
               k_gain=np.asarray(k_gain, np.float32), w_out_odd=np.asarray(w_out_odd, np.float32),
               final_norm=np.asarray(final_norm, np.float32))
    return kernel_fused(inp)
```

```python
import os
import numpy as np
import ml_dtypes
import concourse.bass as bass
import concourse.mybir as mybir
from concourse.bass_utils import run_bass_kernel_spmd
from contextlib import ExitStack

F32 = mybir.dt.float32
BF16 = mybir.dt.bfloat16
AF = mybir.ActivationFunctionType
ALU = mybir.AluOpType
AX = mybir.AxisListType
NPBF = ml_dtypes.bfloat16

NCORES = 8
D = 1024
S = 8192
T = 2048
NT = T // 128
EPS = 1e-6


class Prog:
    ENGS = ("pe", "act", "dve", "pool", "sp")

    def __init__(self, nc, ndma_sems=32):
        self.nc = nc
        self.ops = []
        self.per_eng = {e: [] for e in self.ENGS}
        self.last_w = {}
        self.readers = {}
        self.ndma_sems = ndma_sems

    def op(self, eng, fn, reads=(), writes=(), dma=False, nosignal=False, own_sem=None):
        o = dict(eng=eng, fn=fn, dma=dma, deps=set(), idx=len(self.ops), signal=False, nosignal=nosignal, own_sem=own_sem)
        for r in reads:
            w = self.last_w.get(r)
            if w is not None:
                o["deps"].add(w)
        for r in writes:
            w = self.last_w.get(r)
            if w is not None:
                o["deps"].add(w)
            for rd in self.readers.get(r, ()):
                o["deps"].add(rd)
        o["deps"].discard(o["idx"])
        for r in reads:
            self.readers.setdefault(r, []).append(o["idx"])
        for r in writes:
            self.last_w[r] = o["idx"]
            self.readers[r] = []
        self.ops.append(o)
        self.per_eng[eng].append(o)
        return o["idx"]

    def dma(self, eng, out, in_, reads=(), writes=()):
        return self.op(eng, lambda e: e.dma_start(out=out, in_=in_), reads, writes, dma=True)

    def mm(self, out, lhsT, rhs, start, stop, reads=(), writes=()):
        return self.op("pe", lambda e: e.matmul(out, lhsT, rhs, start=start, stop=stop), reads, writes)

    def tr(self, out, in_, ident, reads=(), writes=()):
        return self.op("pe", lambda e: e.transpose(out, in_, ident), reads, writes)

    def act(self, out, in_, func, reads=(), writes=(), eng="act", **kw):
        return self.op(eng, lambda e: e.activation(out, in_, func, **kw), reads, writes)

    def tt(self, eng, out, in0, in1, op, reads=(), writes=()):
        return self.op(eng, lambda e: e.tensor_tensor(out, in0, in1, op), reads, writes)

    def ts(self, eng, out, in0, s1, s2, op0, op1, reads=(), writes=()):
        return self.op(eng, lambda e: e.tensor_scalar(out, in0, s1, s2, op0, op1), reads, writes)

    def tsmul(self, eng, out, in0, s1, reads=(), writes=()):
        return self.op(eng, lambda e: e.tensor_scalar_mul(out, in0, s1), reads, writes)

    def recip(self, out, in_, reads=(), writes=()):
        return self.op("dve", lambda e: e.reciprocal(out, in_), reads, writes)

    def stt(self, eng, out, in0, scalar, in1, op0, op1, reads=(), writes=()):
        return self.op(eng, lambda e: e.scalar_tensor_tensor(out, in0, scalar, in1, op0, op1), reads, writes)

    def copy(self, eng, out, in_, reads=(), writes=()):
        if eng == "act":
            return self.op(eng, lambda e: e.activation(out, in_, AF.Copy), reads, writes)
        return self.op(eng, lambda e: e.tensor_copy(out, in_), reads, writes)

    def memset(self, eng, ap, val, writes=()):
        return self.op(eng, lambda e: e.memset(ap, val), (), writes)

    def emit(self, stack):
        nc = self.nc
        ops = self.ops
        if not hasattr(self, "esem"):
            self.esem = {e: stack.enter_context(nc.semaphore("s_" + e)) for e in self.ENGS}
            self.dsem = [stack.enter_context(nc.semaphore("d%d" % i)) for i in range(self.ndma_sems)]
            self.dcount = [0] * self.ndma_sems
            self.dlast = [None] * self.ndma_sems
            self.rr = 0
            self.ecount = {e: 0 for e in self.ENGS}
            self.waited = {e: {} for e in self.ENGS}
            self.emitted = 0
        esem, dsem = self.esem, self.dsem
        new = ops[self.emitted:]
        for o in new:
            best = {}
            nd = set()
            for d in o["deps"]:
                p = ops[d]
                if p["dma"]:
                    nd.add(d)
                    continue
                if p["eng"] == "pe" and o["eng"] == "pe" and not o["dma"]:
                    continue
                if best.get(p["eng"], -1) < d:
                    best[p["eng"]] = d
            nd.update(best.values())
            o["deps"] = {d for d in nd if ops[d]["dma"] or d >= self.emitted}
        for o in new:
            if o["dma"] and o["own_sem"] is not None:
                o["sem"] = ("x", o["own_sem"], 16)
            elif o["dma"]:
                s = self.rr % self.ndma_sems
                self.rr += 1
                if self.dlast[s] is not None:
                    o["deps"].add(self.dlast[s])
                self.dcount[s] += 16
                o["sem"] = ("d", s, self.dcount[s])
                self.dlast[s] = o["idx"]
        for o in new:
            for d in o["deps"]:
                if not ops[d]["dma"]:
                    ops[d]["signal"] = True
        lastop = {}
        for o in new:
            if not o["dma"]:
                lastop[o["eng"]] = o
        for o in lastop.values():
            o["signal"] = True
        prev_end = dict(getattr(self, "phase_end", {}))
        for o in new:
            if (not o["dma"]) and o["signal"]:
                self.ecount[o["eng"]] += 1
                o["sem"] = ("e", o["eng"], self.ecount[o["eng"]])
        self.phase_end = dict(self.ecount)
        final_waits = [(dsem[s], self.dcount[s]) for s in range(self.ndma_sems) if self.dcount[s] > 0]
        start = self.emitted
        self.emitted = len(ops)
        with nc.Block() as blk:

            def run_engine(ename):
                def body(eng):
                    waited = self.waited[ename]
                    self.rank_cache = {}
                    for pe_, pv_ in prev_end.items():
                        if pv_ > 0 and waited.get(("e", pe_), 0) < pv_:
                            eng.wait_ge(esem[pe_], pv_)
                            waited[("e", pe_)] = pv_
                    for o in self.per_eng[ename]:
                        if o["idx"] < start:
                            continue
                        need = {}
                        for d in o["deps"]:
                            kind, key, val = ops[d]["sem"]
                            k = (kind, key)
                            if need.get(k, 0) < val:
                                need[k] = val
                        for k, val in need.items():
                            if waited.get(k, 0) >= val:
                                continue
                            eng.wait_ge(esem[k[1]] if k[0] == "e" else (dsem[k[1]] if k[0] == "d" else k[1]), val)
                            waited[k] = val
                        ins = o["fn"](eng)
                        if o["dma"] and o["own_sem"] is not None:
                            ins.then_inc(o["own_sem"], 16)
                        elif o["dma"]:
                            ins.then_inc(dsem[o["sem"][1]], 16)
                        elif o["signal"]:
                            if o["nosignal"]:
                                ins = eng.memset(self.nosig_scratch[:, self.nosig_n:self.nosig_n + 1], 0.0)
                                self.nosig_n += 1
                            ins.then_inc(esem[ename], 1)
                    if ename == "sp":
                        for sem, val in final_waits:
                            eng.wait_ge(sem, val)
                return body

            blk.tensor(run_engine("pe"))
            blk.scalar(run_engine("act"))
            blk.vector(run_engine("dve"))
            blk.gpsimd(run_engine("pool"))
            blk.sync(run_engine("sp"))

    def rank4(self, e):
        if "r" not in self.rank_cache:
            self.rank_cache["r"] = e.snap(e.partition_id() % 4, min_val=0, max_val=3)
        return self.rank_cache["r"]

    def end_phase(self, stack):
        last = {}
        for o in self.ops[getattr(self, "emitted", 0):]:
            if not o["dma"]:
                last[o["eng"]] = o
        self.emit(stack)


def _bf(a):
    return np.ascontiguousarray(a.astype(np.float32)).astype(NPBF)


def make_consts():
    c = {}
    c["ident"] = _bf(np.eye(128))
    k = np.arange(128)
    ang = 2 * np.pi * np.outer(k, k) / 128.0
    c["c1s1"] = _bf(np.concatenate([np.cos(ang), -np.sin(ang)], axis=1))
    c["ccsc"] = _bf(np.concatenate([np.cos(ang), np.sin(ang)], axis=1) / 1024.0)
    s2 = np.arange(64)[:, None, None]
    k1 = np.arange(128)[None, :, None]
    k2 = np.arange(64)[None, None, :]
    th = 2 * np.pi * ((s2 * (k1 + 128 * k2)) % 8192) / 8192.0
    c["w2a"] = _bf(np.concatenate([np.cos(th), -np.sin(th)], axis=2).reshape(64, 128 * 128))
    c["w2b"] = _bf(np.concatenate([np.sin(th), np.cos(th)], axis=2).reshape(64, 128 * 128))
    p0 = np.zeros((128, 128), np.float32)
    for i in range(64):
        p0[2 * i + 1, 2 * i] = -1.0
        p0[2 * i, 2 * i + 1] = 1.0
    c["p0"] = _bf(p0)
    c["ones"] = _bf(np.ones((128, 128)))
    return c


def rope_tables(core):
    q = core % 4
    s = np.arange(T) + q * T
    row = (s // 64).astype(np.float32)
    col = (s % 64).astype(np.float32)
    inv = (np.float32(10000.0) ** (-np.arange(32, dtype=np.float32) / np.float32(32))).astype(np.float32)
    ang = np.concatenate([row[:, None] * inv[None, :], col[:, None] * inv[None, :]], axis=1).astype(np.float32)
    cos = np.cos(ang).astype(np.float32)
    sin = np.sin(ang).astype(np.float32)
    cosT = np.repeat(cos.T, 2, axis=0)
    sinT = np.repeat(sin.T, 2, axis=0)
    return np.ascontiguousarray(cosT), np.ascontiguousarray(sinT)


def rmsnorm_to_hT(P, nc, st, xsrc_tile, xres_tok, ntiles, hT, hT_tok, ident, tag, load_fn=None, after_tile=None):
    sb = lambda n, s, d: st.enter_context(nc.sbuf_tensor("s_" + n, s, d))
    junk = [sb(f"{tag}_junk{j}", [128, D], BF16) for j in range(2)]
    xn = [sb(f"{tag}_xn{j}", [128, D], BF16) for j in range(2)]
    ss = sb(f"{tag}_ss", [128, ntiles], F32)
    rstd = sb(f"{tag}_rstd", [128, ntiles], F32)
    epsr = sb(f"{tag}_eps", [128, 1], F32)
    pst = [st.enter_context(nc.psum_tensor(f"p_{tag}_pst{j}", [128, 8, 128], BF16)) for j in range(2)]
    P.memset("dve", ss[:], 0.0, writes=[f"{tag}_ss"])
    P.memset("dve", epsr[:], EPS, writes=[f"{tag}_eps"])

    def stage1(i):
        if load_fn is not None:
            load_fn(i)
        j = i % 2
        xs = xsrc_tile(i)
        P.act(junk[j][:], xs, AF.Square, reads=[xres_tok(i), f"{tag}_ss"], writes=[f"{tag}_junk{j}", f"{tag}_ss{i}"],
              accum_out=ss[:, i:i + 1])
        P.act(rstd[:, i:i + 1], ss[:, i:i + 1], AF.Ln, reads=[f"{tag}_ss{i}", f"{tag}_eps"], writes=[f"{tag}_rstd{i}a"],
              scale=1.0 / D, bias=epsr[:, 0:1])
        P.act(rstd[:, i:i + 1], rstd[:, i:i + 1], AF.Exp, reads=[f"{tag}_rstd{i}a"], writes=[f"{tag}_rstd{i}"], scale=-0.5)

    def stage2(i):
        j = i % 2
        xs = xsrc_tile(i)
        P.act(xn[j][:], xs, AF.Copy, reads=[xres_tok(i), f"{tag}_rstd{i}"], writes=[f"{tag}_xn{j}"],
              scale=rstd[:, i:i + 1])
        for k in range(8):
            P.tr(pst[j][:, k, :], xn[j][:, k * 128:(k + 1) * 128], ident[:],
                 reads=[f"{tag}_xn{j}", "ident"], writes=[f"{tag}_pst{j}"])
        P.copy("dve", hT(i), pst[j][:], reads=[f"{tag}_pst{j}"], writes=[hT_tok(i)])

    for i in range(ntiles + 2):
        if i < ntiles:
            stage1(i)
        if 1 <= i <= ntiles:
            stage2(i - 1)
        if i >= 2 and after_tile is not None:
            after_tile(i - 2)


def build(mode):
    nc = bass.Bass("TRN2", target_bir_lowering=False)
    dr = lambda n, s, d, k: nc.dram_tensor(n, s, d, kind=k).ap()
    IN, OUT = "ExternalInput", "ExternalOutput"
    with ExitStack() as st:
        sb = lambda n, s, d: st.enter_context(nc.sbuf_tensor("s_" + n, s, d))
        ps = lambda n, s, d=F32: st.enter_context(nc.psum_tensor("p_" + n, s, d))
        P = Prog(nc)
        if mode == "A":
            phaseA(P, nc, st, sb, ps, dr, IN, OUT)
        elif mode == "B":
            phaseB(P, nc, st, sb, ps, dr, IN, OUT)
        elif mode == "C":
            phaseC(P, nc, st, sb, ps, dr, IN, OUT)
        elif mode == "E":
            phaseE(P, nc, st, sb, ps, dr, IN, OUT)
        elif mode == "D":
            phaseD(P, nc, st, sb, ps, dr, IN, OUT)
        P.emit(st)
    return nc


def stream_w_chunks(P, nc, sb, w_dram, gfull, ncols, order, tag, nstage=3, nbuf=3, engs=("dve", "pool"), eager=False, ahead=None):
    stage = [sb(f"{tag}_stg{j}", [128, 8, 128], F32) for j in range(nstage)]
    wb = [sb(f"{tag}_wb{j}", [128, 8, 128], BF16) for j in range(nbuf)]
    wv = w_dram.rearrange("(k p) n -> p k n", p=128)
    issued = [0]

    def issue(n):
        cid = order[n]
        sj, bj = n % nstage, n % nbuf
        P.dma("sp", stage[sj][:], wv[:, :, cid * 128:(cid + 1) * 128], writes=[f"{tag}_stg{sj}"])
        eng = engs[n % len(engs)]
        P.tt(eng, wb[bj][:], stage[sj][:], gfull[:], ALU.mult, reads=[f"{tag}_stg{sj}", "gfull"], writes=[f"{tag}_wb{bj}"])

    for n, cid in enumerate(order):
        if ahead is None:
            ahead = (nbuf - 1) if eager else 0
        while issued[0] <= min(n + ahead, len(order) - 1):
            issue(issued[0])
            issued[0] += 1
        yield (cid, wb[n % nbuf], f"{tag}_wb{n % nbuf}")


def phaseA(P, nc, st, sb, ps, dr, IN, OUT):
    x = dr("x", [T, D], F32, IN)
    xh = dr("xh", [2, D], F32, IN)
    g0 = dr("g0", [128, 8], F32, IN)
    w_in = dr("w_in", [D, 3072], F32, IN)
    cw = dr("cw", [128, 12], F32, IN)
    identd = dr("ident", [128, 128], BF16, IN)
    u_out = dr("u", [T, 512], BF16, OUT)
    ya_out = dr("yaT", [4, 128, T], BF16, OUT)
    sbz_out = dr("sbzT", [4, 128, T], BF16, OUT)

    ident = sb("ident", [128, 128], BF16)
    g0s = sb("g0s", [128, 8], F32)
    cws = sb("cws", [128, 12], F32)
    gfull = sb("gfull", [128, 8, 128], F32)
    hT = sb("hT", [128, 8, T], BF16)
    hTh = sb("hTh", [128, 8, 128], BF16)
    xt = [sb(f"xt{j}", [128, D], F32) for j in range(3)]
    P.dma("sp", ident[:], identd, writes=["ident"])
    P.dma("sp", g0s[:], g0, writes=["g0s"])
    P.dma("sp", cws[:], cw, writes=["cws"])
    P.memset("pool", gfull[:], 1.0, writes=["gfull"])
    for k in range(8):
        P.tsmul("pool", gfull[:, k, :], gfull[:, k, :], g0s[:, k:k + 1], reads=["g0s", "gfull"], writes=["gfull"])

    xv = x.rearrange("(i p) d -> i p d", p=128)

    def load(i):
        j = i % 3
        if i < NT:
            P.dma("sp", xt[j][:], xv[i], writes=[f"xt{j}"])
        else:
            P.memset("pool", xt[j][:], 0.0, writes=[f"xt{j}"])
            P.dma("sp", xt[j][0:2, :], xh, writes=[f"xt{j}"])

    if True:
        rmsnorm_to_hT(P, nc, st, lambda i: xt[i % 3][:], lambda i: f"xt{i % 3}", NT + 1,
                      lambda i: (hT[:, :, i * 128:(i + 1) * 128] if i < NT else hTh[:, :, :]),
                      lambda i: (f"hT{i // 4}" if i < NT else "hTh"), ident, "rn", load_fn=load)

        order = [16, 17, 18, 19]
        for j in range(4):
            order += [j, 8 + j, 4 + j, 12 + j]
        order += [20, 21, 22, 23]
        pp = [ps(f"pp{j}", [128, 512]) for j in range(4)]
        ph = ps("ph", [128, 2, 2])
        u_sb = sb("u_sb", [128, NT, 512], BF16)
        tbuf = sb("tbuf", [128, T + 2], F32)
        cbuf = sb("cbuf", [128, T], F32)
        axs = [sb(f"axs{j}", [128, 512], F32) for j in range(2)]
        szs = [sb(f"szs{j}", [128, 512], F32) for j in range(2)]
        vs = [sb(f"vs{j}", [128, 512], F32) for j in range(2)]
        hprod = sb("hprod", [128, 2, 2], F32)
        yaT = sb("yaT", [128, 4, T], BF16)
        sbzT = sb("sbzT", [128, 4, T], BF16)
        ppi = [0]

        def proj(wb, wtok, tb):
            j = ppi[0] % 4
            ppi[0] += 1
            for k in range(8):
                P.mm(pp[j][:], wb[:, k, :], hT[:, k, tb * 512:(tb + 1) * 512], k == 0, k == 7,
                     reads=[wtok, f"hT{tb}"], writes=[f"pp{j}"])
            return pp[j], f"pp{j}"

        chunks = {}
        n_u = 0
        for cid, wb, wtok in stream_w_chunks(P, nc, sb, w_in, gfull, 3072, order, "wi"):
            kind, j = cid // 4, cid % 4
            if kind == 4:
                for i in range(NT):
                    pj = ppi[0] % 4
                    ppi[0] += 1
                    for k in range(8):
                        P.mm(pp[pj][:, 0:128], hT[:, k, i * 128:(i + 1) * 128], wb[:, k, :], k == 0, k == 7,
                             reads=[wtok, f"hT{i // 4}"], writes=[f"pp{pj}"])
                    P.copy("act", u_sb[:, i, j * 128:(j + 1) * 128], pp[pj][:, 0:128], reads=[f"pp{pj}"], writes=[f"u_sb{j}"])
                if j == 3:
                    P.dma("act", u_out.rearrange("(i p) c -> p i c", p=128), u_sb[:], reads=[f"u_sb{jj}" for jj in range(4)])
            elif kind == 5:
                for tb in range(4):
                    pt, ptok = proj(wb, wtok, tb)
                    P.act(sbzT[:, j, tb * 512:(tb + 1) * 512], pt[:], AF.Silu, reads=[ptok], writes=[f"sbzT{j}"])
                P.dma("act", sbz_out[j], sbzT[:, j, :], reads=[f"sbzT{j}"])
            elif kind == 0:
                chunks["x"] = (wb, wtok)
            elif kind == 2:
                wbx, wtokx = chunks["x"]
                for tb in range(4):
                    ptx, ptokx = proj(wbx, wtokx, tb)
                    a = tb % 2
                    P.copy("act", axs[a][:], ptx[:], reads=[ptokx], writes=[f"axs{a}"])
                    ptc, ptokc = proj(wb, wtok, tb)
                    P.tt("dve", tbuf[:, 1 + tb * 512:1 + (tb + 1) * 512], ptc[:], axs[a][:], ALU.mult,
                         reads=[ptokc, f"axs{a}"], writes=[f"tbuf{tb}"])
                for k in range(8):
                    P.mm(ph[:, 0, :], wbx[:, k, :], hTh[:, k, 0:2], k == 0, k == 7, reads=[wtokx, "hTh"], writes=["ph0"])
                for k in range(8):
                    P.mm(ph[:, 1, :], wb[:, k, :], hTh[:, k, 0:2], k == 0, k == 7, reads=[wtok, "hTh"], writes=["ph1"])
                P.copy("act", hprod[:, 0, :], ph[:, 0, :], reads=["ph0"], writes=["hprod0"])
                P.tt("dve", hprod[:, 1, :], ph[:, 1, :], hprod[:, 0, :], ALU.mult, reads=["ph1", "hprod0"], writes=["hprod1"])
                P.copy("dve", tbuf[:, 0:1], hprod[:, 1, 0:1], reads=["hprod1"], writes=["tbufL"])
                P.copy("dve", tbuf[:, T + 1:T + 2], hprod[:, 1, 1:2], reads=["hprod1"], writes=["tbufR"])
                alltb = [f"tbuf{tb}" for tb in range(4)]
                P.act(cbuf[:], tbuf[:, 1:T + 1], AF.Copy, reads=alltb, writes=["cbuf"], scale=cws[:, 3 * j + 1:3 * j + 2])
                P.stt("dve", cbuf[:], tbuf[:, 0:T], cws[:, 3 * j:3 * j + 1], cbuf[:], ALU.mult, ALU.add,
                      reads=alltb + ["tbufL", "cws"], writes=["cbuf"])
                P.stt("dve", cbuf[:], tbuf[:, 2:T + 2], cws[:, 3 * j + 2:3 * j + 3], cbuf[:], ALU.mult, ALU.add,
                      reads=alltb + ["tbufR", "cws"], writes=["cbuf"])
            elif kind == 1:
                chunks["b"] = (wb, wtok)
            elif kind == 3:
                wbb, wtokb = chunks["b"]
                for tb in range(4):
                    a = tb % 2
                    ptz, ptokz = proj(wb, wtok, tb)
                    P.act(szs[a][:], ptz[:], AF.Silu, reads=[ptokz], writes=[f"szs{a}"])
                    ptb, ptokb = proj(wbb, wtokb, tb)
                    P.tt("dve", vs[a][:], ptb[:], szs[a][:], ALU.mult, reads=[ptokb, f"szs{a}"], writes=[f"vs{a}"])
                    P.tt("pool", yaT[:, j, tb * 512:(tb + 1) * 512], cbuf[:, tb * 512:(tb + 1) * 512], vs[a][:], ALU.mult,
                         reads=["cbuf", f"vs{a}"], writes=[f"yaT{j}"])
                P.dma("pool", ya_out[j], yaT[:, j, :], reads=[f"yaT{j}"])


def halo_rows(x, core):
    b, q = core // 4, core % 4
    lo, hi = q * T, (q + 1) * T
    xh = np.zeros((2, D), np.float32)
    if lo > 0:
        xh[0] = x[b, lo - 1]
    if hi < S:
        xh[1] = x[b, hi]
    return xh


def inputs_A(inp):
    c = make_consts()
    x = inp["x"]
    g0 = np.ascontiguousarray(inp["norm_even"][0].reshape(8, 128).T)
    cw = np.ascontiguousarray(inp["conv_w"][0].reshape(3, 4, 128).transpose(2, 1, 0).reshape(128, 12))
    w_in = np.ascontiguousarray(inp["w_in_even"][0])
    maps = []
    for core in range(NCORES):
        b, q = core // 4, core % 4
        maps.append(dict(x=np.ascontiguousarray(x[b, q * T:(q + 1) * T]), xh=halo_rows(x, core), g0=g0,
                         w_in=w_in, cw=cw, ident=c["ident"]))
    return maps


def fft_consts_packed(c):
    w2 = np.concatenate([np.asarray(c["w2a"]), np.asarray(c["w2b"])], axis=0)
    return np.ascontiguousarray(w2)


def phaseB(P, nc, st, sb, ps, dr, IN, OUT):
    u = dr("ubg", [S, 128], BF16, IN)
    c1s1d = dr("c1s1", [128, 256], BF16, IN)
    w2d = dr("w2", [128, 128 * 128], BF16, IN)
    ccscd = dr("ccsc", [128, 256], BF16, IN)
    ft = dr("ft", [128, S], F32, OUT)
    fft_core(P, nc, sb, ps, u, c1s1d, w2d, ccscd, ft)


def fft_core(P, nc, sb, ps, u, c1s1d, w2d, ccscd, ft):
    X0 = sb("X0", [128, 2, 64, 128], BF16)
    X1 = sb("X1", [128, 128, 128], BF16)
    W2 = sb("W2", [128, 128, 128], BF16)
    X2 = sb("X2", [128, 2, S], BF16)
    X2r = X2[:, 0, :]
    X2i = X2[:, 1, :]
    c1s1 = sb("c1s1", [128, 256], BF16)
    ccsc = sb("ccsc", [128, 256], BF16)
    fo = [sb(f"fo{j}", [128, 2048], F32) for j in range(2)]
    pb = [ps(f"pb{j}", [128, 512]) for j in range(4)]
    uv = u.rearrange("(p j) c -> p j c", j=64)
    P.dma("sp", c1s1[:], c1s1d, writes=["c1s1"])
    P.dma("sp", X0[:, 0, :, :], uv, writes=["X0a"])
    P.dma("pool", X0[:, 1, :, :], uv, writes=["X0b"])
    P.dma("sp", ccsc[:], ccscd, writes=["ccsc"])
    w2v = w2d.rearrange("p (a b) -> p a b", b=128)
    for h in range(4):
        P.dma("sp" if h % 2 == 0 else "pool", W2[:, h * 32:(h + 1) * 32, :], w2v[:, h * 32:(h + 1) * 32, :], writes=[f"W2_{h}"])
    n = 0
    for c0 in range(0, int(os.environ.get("FFT_NC", "128")), 2):
        j = n % 4
        n += 1
        pt = pb[j][:].rearrange("p (a b) -> p a b", b=256)
        for cc in range(2):
            c = c0 + cc
            P.mm(pt[:, cc, :], X0[:, :, :, c], c1s1[:], True, True, reads=["X0a", "X0b", "c1s1"], writes=[f"pb{j}"])
        if os.environ.get("FFT_NOCOPY") == "1":
            continue
        P.copy("dve", X1[0:64, c0:c0 + 2, :], pt[0:64, :, 0:128], reads=[f"pb{j}"], writes=[f"X1r{c0}"])
        if os.environ.get("FFT_NOCOPY") == "2":
            continue
        P.copy("act", X1[64:128, c0:c0 + 2, :], pt[64:128, :, 128:256], reads=[f"pb{j}"], writes=[f"X1i{c0}"])
    if os.environ.get("FFT_STEPS") == "1":
        P.dma("sp", ft[:, 0:2048], fo[0][:], reads=[f"X1r{c0}" for c0 in range(0, 128, 2)] + [f"X1i{c0}" for c0 in range(0, 128, 2)])
        return
    x1all = [f"X1r{c0}" for c0 in range(0, 128, 2)] + [f"X1i{c0}" for c0 in range(0, 128, 2)]
    for k0 in range(0, int(os.environ.get("FFT_NK", "128")), 4):
        j = n % 4
        n += 1
        pt = pb[j][:].rearrange("p (a b) -> p a b", b=128)
        for kk in range(4):
            k1 = k0 + kk
            P.mm(pt[:, kk, :], X1[:, :, k1], W2[:, k1, :], True, True,
                 reads=x1all + [f"W2_{k1 // 32}"], writes=[f"pb{j}"])
        X2v = X2[:].rearrange("p r (k1 k2) -> p k1 r k2", k2=64)[:, k0:k0 + 4, :, :]
        P.copy("dve" if (k0 // 4) % 2 == 0 else "act", X2v, pb[j][:].rearrange("p (a r b) -> p a r b", r=2, b=64),
               reads=[f"pb{j}"], writes=[f"X2r{k0}", f"X2i{k0}"])
    if os.environ.get("FFT_STEPS") == "2":
        P.dma("sp", ft[:, 0:2048], fo[0][:], reads=[f"X2r{k0}" for k0 in range(0, 128, 4)] + [f"X2i{k0}" for k0 in range(0, 128, 4)])
        return
    x2all = [f"X2r{k0}" for k0 in range(0, 128, 4)] + [f"X2i{k0}" for k0 in range(0, 128, 4)]
    for kb in range(16):
        j = n % 4
        n += 1
        rr_ = X2r.rearrange("p (k1 k2) -> p k2 k1", k2=64)[:, 4 * kb:4 * kb + 4, :]
        ri_ = X2i.rearrange("p (k1 k2) -> p k2 k1", k2=64)[:, 4 * kb:4 * kb + 4, :]
        P.mm(pb[j][:], ccsc[:, 0:128], rr_, True, False, reads=x2all + ["ccsc"], writes=[f"pb{j}"])
        P.mm(pb[j][:], ccsc[:, 128:256], ri_, False, True, reads=x2all + ["ccsc"], writes=[f"pb{j}"])
        f = kb // 4
        P.copy("dve" if kb % 2 == 0 else "act", fo[f % 2][:, (kb % 4) * 512:(kb % 4 + 1) * 512], pb[j][:],
               reads=[f"pb{j}"], writes=[f"fo{f % 2}_{kb % 4}"])
        if kb % 4 == 3:
            P.dma("sp", ft[:, f * 2048:(f + 1) * 2048], fo[f % 2][:], reads=[f"fo{f % 2}_{q}" for q in range(4)])


def inputs_B(u_full):
    c = make_consts()
    w2 = fft_consts_packed(c)
    maps = []
    for core in range(NCORES):
        b, g = core // 4, core % 4
        maps.append(dict(ubg=np.ascontiguousarray(u_full[b, :, g * 128:(g + 1) * 128]), c1s1=c["c1s1"], w2=w2, ccsc=c["ccsc"]))
    return maps


def phaseC(P, nc, st, sb, ps, dr, IN, OUT):
    x = dr("x", [T, D], F32, IN)
    ftd = dr("ft", [4, 128, T], F32, IN)
    yad = dr("yaT", [4, 128, T], BF16, IN)
    sbzd = dr("sbzT", [4, 128, T], BF16, IN)
    wod = dr("w_out", [D, D], F32, IN)
    x1 = dr("x1", [T, D], F32, OUT)
    yT = sb("yT", [128, 8, T], BF16)
    sbz = sb("sbz", [128, 4, T], BF16)
    fst = [sb(f"fst{j}", [128, T], BF16) for j in range(2)]
    Wo = sb("Wo", [128, 8, D], BF16)
    wst = [sb(f"wst{j}", [128, D], F32) for j in range(2)]
    xt = [sb(f"xt{j}", [128, D], F32) for j in range(3)]
    pso = [ps(f"pso{j}", [128, 512]) for j in range(4)]
    for g in range(4):
        P.dma("sp", yT[:, g, :], yad[g], writes=[f"yT{g}"])
        P.dma("pool", sbz[:, g, :], sbzd[g], writes=[f"sbz{g}"])
    for g in range(4):
        P.dma("sp", fst[g % 2][:], ftd[g], writes=[f"fst{g % 2}"])
        P.tt("dve" if g % 2 == 0 else "pool", yT[:, 4 + g, :], fst[g % 2][:], sbz[:, g, :], ALU.mult,
             reads=[f"fst{g % 2}", f"sbz{g}"], writes=[f"yT{4 + g}"])
    for kc in range(8):
        P.dma("sp", wst[kc % 2][:], wod[kc * 128:(kc + 1) * 128, :], writes=[f"wst{kc % 2}"])
        P.copy("dve" if kc % 2 == 0 else "pool", Wo[:, kc, :], wst[kc % 2][:], reads=[f"wst{kc % 2}"], writes=[f"Wo{kc}"])
    xv = x.rearrange("(i p) d -> i p d", p=128)
    x1v = x1.rearrange("(i p) d -> i p d", p=128)
    n = 0
    for i in range(NT):
        j3 = i % 3
        P.dma("sp", xt[j3][:], xv[i], writes=[f"xt{j3}"])
        for nb in range(2):
            j = n % 4
            n += 1
            for kc in range(8):
                P.mm(pso[j][:], yT[:, kc, i * 128:(i + 1) * 128], Wo[:, kc, nb * 512:(nb + 1) * 512], kc == 0, kc == 7,
                     reads=[f"yT{kc}", f"Wo{kc}"], writes=[f"pso{j}"])
            P.tt("dve", xt[j3][:, nb * 512:(nb + 1) * 512], pso[j][:], xt[j3][:, nb * 512:(nb + 1) * 512], ALU.add,
                 reads=[f"pso{j}", f"xt{j3}"], writes=[f"xt{j3}"])
        P.dma("act", x1v[i], xt[j3][:], reads=[f"xt{j3}"])


def inputs_C(inp, ft_full, ya, sbz):
    maps = []
    w_out = np.ascontiguousarray(inp["w_out_even"][0])
    for core in range(NCORES):
        b, q = core // 4, core % 4
        ft = np.ascontiguousarray(ft_full[b * 4:(b + 1) * 4, :, q * T:(q + 1) * T])
        maps.append(dict(x=np.ascontiguousarray(inp["x"][b, q * T:(q + 1) * T]), ft=ft, yaT=ya[core], sbzT=sbz[core], w_out=w_out))
    return maps


def phaseE(P, nc, st, sb, ps, dr, IN, OUT):
    x1 = dr("x1", [T, D], F32, IN)
    g1 = dr("g1", [128, 8], F32, IN)
    w_in = dr("w_in", [D, 2560], F32, IN)
    gv = dr("gv", [128, 4], F32, IN)
    cosd = dr("cos", [128, T], F32, IN)
    sind = dr("sin", [128, T], F32, IN)
    identd = dr("ident", [128, 128], BF16, IN)
    p0d = dr("p0", [128, 128], BF16, IN)
    onesd = dr("ones", [128, 128], BF16, IN)
    qT_out = dr("qT", [8, 128, T], BF16, OUT)
    kT_out = dr("kT", [2, 128, T], BF16, OUT)
    vx_out = dr("vx", [T, 258], BF16, OUT)
    sz_out = dr("szT", [8, 128, T], BF16, OUT)

    ident = sb("ident", [128, 128], BF16)
    p0 = sb("p0", [128, 128], BF16)
    ones = sb("ones", [128, 128], BF16)
    g1s = sb("g1s", [128, 8], F32)
    gvs = sb("gvs", [128, 4], F32)
    gfull = sb("gfull", [128, 8, 128], F32)
    cos = sb("cos", [128, T], F32)
    sin = sb("sin", [128, T], F32)
    hT = sb("hT", [128, 8, T], BF16)
    xt = [sb(f"xt{j}", [128, D], F32) for j in range(3)]
    for t_, d_, n_ in ((ident, identd, "ident"), (p0, p0d, "p0"), (ones, onesd, "ones"), (g1s, g1, "g1s"), (gvs, gv, "gvs"),
                       (cos, cosd, "cos"), (sin, sind, "sin")):
        P.dma("sp", t_[:], d_, writes=[n_])
    P.memset("pool", gfull[:], 1.0, writes=["gfull"])
    for k in range(8):
        P.tsmul("pool", gfull[:, k, :], gfull[:, k, :], g1s[:, k:k + 1], reads=["g1s", "gfull"], writes=["gfull"])
    xv = x1.rearrange("(i p) d -> i p d", p=128)

    def load(i):
        P.dma("sp", xt[i % 3][:], xv[i], writes=[f"xt{i % 3}"])

    rmsnorm_to_hT(P, nc, st, lambda i: xt[i % 3][:], lambda i: f"xt{i % 3}", NT,
                  lambda i: hT[:, :, i * 128:(i + 1) * 128], lambda i: f"hT{i // 4}", ident, "rn", load_fn=load)

    pp = [ps(f"pp{j}", [128, 512]) for j in range(4)]
    pss = ps("pss", [128, 512])
    prot = ps("prot", [128, 512])
    qb = [sb(f"qb{j}", [128, 512], BF16) for j in range(2)]
    sq = [sb(f"sq{j}", [128, 512], BF16) for j in range(2)]
    rs = [sb(f"rs{j}", [128, 512], F32) for j in range(2)]
    ta = [sb(f"ta{j}", [128, 512], F32) for j in range(2)]
    tb_ = [sb(f"tb{j}", [128, 512], F32) for j in range(2)]
    obuf = [sb(f"obuf{j}", [128, T], BF16) for j in range(2)]
    vx_sb = sb("vx_sb", [128, NT, 2, 129], BF16)
    P.memset("pool", vx_sb[:], 1.0, writes=["vx_ones"])
    ppi = [0]
    cnt = [0]
    nob = [0]

    def proj(wb, wtok, tb):
        j = ppi[0] % 4
        ppi[0] += 1
        for k in range(8):
            P.mm(pp[j][:], wb[:, k, :], hT[:, k, tb * 512:(tb + 1) * 512], k == 0, k == 7,
                 reads=[wtok, f"hT{tb}"], writes=[f"pp{j}"])
        return pp[j], f"pp{j}"

    order = [8, 9, 10, 11] + list(range(8)) + list(range(12, 20))
    for cid, wb, wtok in stream_w_chunks(P, nc, sb, w_in, gfull, 2560, order, "wi"):
        if cid < 10:
            isq = cid < 8
            gcol = 0 if isq else 2
            ob = nob[0] % 2
            nob[0] += 1
            for tb in range(4):
                a = cnt[0] % 2
                cnt[0] += 1
                pt, ptok = proj(wb, wtok, tb)
                sl = slice(tb * 512, (tb + 1) * 512)
                P.copy("act", qb[a][:], pt[:], reads=[ptok], writes=[f"qb{a}", f"lk1_{a}"])
                P.act(sq[a][:], pt[:], AF.Square, reads=[ptok], writes=[f"sq{a}", f"lk2_{a}"])
                P.mm(pss[:], ones[:], sq[a][:], True, True, reads=["ones", f"sq{a}"], writes=["pss"])
                P.mm(prot[:], p0[:], qb[a][:], True, True, reads=["p0", f"qb{a}"], writes=["prot"])
                P.ts("dve", rs[a][:], pss[:], 1.0 / 128, EPS, ALU.mult, ALU.add, reads=["pss"], writes=[f"rs{a}"])
                P.act(rs[a][:], rs[a][:], AF.Sqrt, reads=[f"rs{a}"], writes=[f"rs{a}"])
                P.recip(rs[a][:], rs[a][:], reads=[f"rs{a}"], writes=[f"rs{a}"])
                P.stt("dve", ta[a][:], pt[:], gvs[:, gcol:gcol + 1], cos[:, sl], ALU.mult, ALU.mult,
                      reads=[ptok, f"lk1_{a}", f"lk2_{a}", "gvs", "cos"], writes=[f"ta{a}"])
                P.stt("dve", tb_[a][:], prot[:], gvs[:, gcol + 1:gcol + 2], sin[:, sl], ALU.mult, ALU.mult,
                      reads=["prot", "gvs", "sin"], writes=[f"tb{a}"])
                P.tt("pool", ta[a][:], ta[a][:], tb_[a][:], ALU.add, reads=[f"ta{a}", f"tb{a}"], writes=[f"ta{a}"])
                P.tt("pool", obuf[ob][:, sl], ta[a][:], rs[a][:], ALU.mult, reads=[f"ta{a}", f"rs{a}"], writes=[f"obuf{ob}"])
            dst = qT_out[cid] if isq else kT_out[cid - 8]
            P.dma("pool", dst, obuf[ob][:], reads=[f"obuf{ob}"])
        elif cid < 12:
            kvh = cid - 10
            for i in range(NT):
                j = ppi[0] % 4
                ppi[0] += 1
                for k in range(8):
                    P.mm(pp[j][:, 0:128], hT[:, k, i * 128:(i + 1) * 128], wb[:, k, :], k == 0, k == 7,
                         reads=[wtok, f"hT{i // 4}"], writes=[f"pp{j}"])
                P.copy("act", vx_sb[:, i, kvh, 0:128], pp[j][:, 0:128], reads=[f"pp{j}", "vx_ones"], writes=[f"vx{kvh}"])
            if kvh == 1:
                P.dma("act", vx_out.rearrange("(i p) c -> p i c", p=128), vx_sb[:].rearrange("p i a b -> p i (a b)"),
                      reads=["vx0", "vx1"])
        else:
            ob = nob[0] % 2
            nob[0] += 1
            for tb in range(4):
                pt, ptok = proj(wb, wtok, tb)
                P.act(obuf[ob][:, tb * 512:(tb + 1) * 512], pt[:], AF.Silu, reads=[ptok], writes=[f"obuf{ob}"])
            P.dma("act", sz_out[cid - 12], obuf[ob][:], reads=[f"obuf{ob}"])


def inputs_E(inp, x1_cores):
    c = make_consts()
    g1 = np.ascontiguousarray(inp["norm_odd"][0].reshape(8, 128).T)
    gq = inp["q_gain"][0]
    gk = inp["k_gain"][0]
    partner = np.arange(128) ^ 1
    gv = np.ascontiguousarray(np.stack([gq, gq[partner], gk, gk[partner]], axis=1).astype(np.float32))
    w_in = np.ascontiguousarray(inp["w_in_odd"][0])
    maps = []
    for core in range(NCORES):
        cosT, sinT = rope_tables(core)
        maps.append(dict(x1=x1_cores[core], g1=g1, w_in=w_in, gv=gv, cos=cosT, sin=sinT,
                         ident=c["ident"], p0=c["p0"], ones=c["ones"]))
    return maps


def phaseD(P, nc, st, sb, ps, dr, IN, OUT):
    qTd = dr("qT", [8, 128, T], BF16, IN)
    kTd = dr("kTf", [2, 128, S], BF16, IN)
    vxd = dr("vxf", [S, 258], BF16, IN)
    szd = dr("szT", [8, 128, T], BF16, IN)
    x1 = dr("x1", [T, D], F32, IN)
    wod = dr("w_out", [D, D], F32, IN)
    fnbd = dr("fnb", [128, D], F32, IN)
    gbd = dr("gb", [128, 2, 128], F32, IN)
    identd = dr("ident", [128, 128], BF16, IN)
    out = dr("out", [T, D], F32, OUT)

    ident = sb("ident", [128, 128], BF16)
    gb = sb("gb", [128, 2, 128], F32)
    fnb = sb("fnb", [128, D], F32)
    mx = sb("mx", [128, 4], F32)
    kT = sb("kT", [128, 2, S], BF16)
    V = sb("V", [128, 64, 258], BF16)
    qT = sb("qT", [128, 8, T], BF16)
    szT = sb("szT", [128, 8, T], BF16)
    Wo = sb("Wo", [128, 8, D], BF16)
    wst = [sb(f"wst{j}", [128, D], F32) for j in range(2)]
    PT = [sb(f"PT{j}", [128, 2, 512], BF16) for j in range(3)]
    on = [sb(f"on{j}", [128, 128], BF16) for j in range(2)]
    rden = sb("rden", [128, 8], F32)
    xt = [sb(f"xt{j}", [128, D], F32) for j in range(2)]
    junk = sb("junk", [128, D], BF16)
    ss = sb("ss", [128, NT], F32)
    pS = [ps(f"pS{j}", [128, 2, 512]) for j in range(2)]
    pO = [ps(f"pO{j}", [128, 2, 129]) for j in range(2)]
    pT = ps("pT", [128, 4, 128], BF16)

    P.dma("sp", ident[:], identd, writes=["ident"])
    P.dma("sp", gb[:], gbd, writes=["gb"])
    P.dma("sp", fnb[:], fnbd, writes=["fnb"])
    for g in range(2):
        P.dma("sp", kT[:, g, :], kTd[g], writes=[f"kT{g}"])
    vv = vxd.rearrange("(c p) f -> p c f", p=128)
    for c4 in range(4):
        P.dma("pool", V[:, c4 * 16:(c4 + 1) * 16, :], vv[:, c4 * 16:(c4 + 1) * 16, :], writes=[f"V{c4}"])
    for h in range(8):
        P.dma("sp", qT[:, h, :], qTd[h], writes=[f"q{h}_{qb}" for qb in range(4)])
        P.dma("pool", szT[:, h, :], szd[h], writes=[f"sz{h}"])
    for kc in range(8):
        P.dma("sp", wst[kc % 2][:], wod[kc * 128:(kc + 1) * 128, :], writes=[f"wst{kc % 2}"])
        P.copy("pool", Wo[:, kc, :], wst[kc % 2][:], reads=[f"wst{kc % 2}"], writes=[f"Wo{kc}"])
    scale = 128.0 ** -0.5
    P.op("dve", lambda e: e.reduce_max(mx[:, 0:1], gb[:, 0, :], AX.X, apply_absolute_value=True), reads=["gb"], writes=["mx0"])
    P.op("dve", lambda e: e.reduce_max(mx[:, 1:2], gb[:, 1, :], AX.X, apply_absolute_value=True), reads=["gb"], writes=["mx1"])
    P.tt("dve", mx[:, 2:3], mx[:, 0:1], mx[:, 1:2], ALU.mult, reads=["mx0", "mx1"], writes=["mx2"])
    P.ts("dve", mx[:, 3:4], mx[:, 2:3], -scale * 128.0, 0.0, ALU.mult, ALU.add, reads=["mx2"], writes=["nbias"])

    Vv = V[:].rearrange("p c (g f) -> p c g f", g=2)
    nS = 0
    for h in range(8):
        g = h // 4
        for qb in range(4):
            qsl = slice(qb * 512, (qb + 1) * 512)
            for kc2 in range(32):
                j = nS % 2
                jp = nS % 3
                nS += 1
                for c2 in range(2):
                    kc = 2 * kc2 + c2
                    P.mm(pS[j][:, c2, :], kT[:, g, kc * 128:(kc + 1) * 128], qT[:, h, qsl], True, True,
                         reads=[f"kT{g}", f"q{h}_{qb}"], writes=[f"pS{j}"])
                P.act(PT[jp][:], pS[j][:], AF.Exp, reads=[f"pS{j}", "nbias"], writes=[f"PT{jp}"], scale=scale, bias=mx[:, 3:4])
                for c2 in range(2):
                    kc = 2 * kc2 + c2
                    for qt in range(4):
                        P.mm(pO[qt // 2][:, qt % 2, :], PT[jp][:, c2, qt * 128:(qt + 1) * 128], Vv[:, kc, g, :],
                             kc == 0, kc == 63, reads=[f"PT{jp}", f"V{kc // 16}"], writes=[f"pO{qt}"])
            for qt in range(4):
                a = qt % 2
                acc = pO[qt // 2][:, qt % 2, :]
                col = (h * 4 + qb) % 2 * 4 + qt
                P.recip(rden[:, col:col + 1], acc[:, 128:129], reads=[f"pO{qt}"], writes=[f"rden{col}"])
                P.tsmul("dve", on[a][:], acc[:, 0:128], rden[:, col:col + 1], reads=[f"pO{qt}", f"rden{col}"], writes=[f"on{a}"])
                P.tr(pT[:, qt, :], on[a][:], ident[:], reads=[f"on{a}", "ident"], writes=["pT"])
            P.tt("dve", qT[:, h, qsl], pT[:].rearrange("p a b -> p (a b)"), szT[:, h, qsl], ALU.mult,
                 reads=["pT", f"sz{h}"], writes=[f"q{h}_{qb}"])

    xv = x1.rearrange("(i p) d -> i p d", p=128)
    ov = out.rearrange("(i p) d -> i p d", p=128)
    P.memset("dve", ss[:], 0.0, writes=["ss"])
    for i in range(NT):
        j2 = i % 2
        P.dma("sp", xt[j2][:], xv[i], writes=[f"xt{j2}"])
        for nb in range(2):
            j = nS % 2
            nS += 1
            for h in range(8):
                P.mm(pS[j][:, 0, :], qT[:, h, i * 128:(i + 1) * 128], Wo[:, h, nb * 512:(nb + 1) * 512], h == 0, h == 7,
                     reads=[f"q{h}_{i // 4}", f"Wo{h}"], writes=[f"pS{j}"])
            P.tt("dve", xt[j2][:, nb * 512:(nb + 1) * 512], pS[j][:, 0, :], xt[j2][:, nb * 512:(nb + 1) * 512], ALU.add,
                 reads=[f"pS{j}", f"xt{j2}"], writes=[f"xt{j2}"])
        P.act(junk[:], xt[j2][:], AF.Square, reads=[f"xt{j2}", "ss"], writes=["junk", f"ss{i}"], accum_out=ss[:, i:i + 1])
        P.ts("dve", ss[:, i:i + 1], ss[:, i:i + 1], 1.0 / D, EPS, ALU.mult, ALU.add, reads=[f"ss{i}"], writes=[f"ss{i}"])
        P.act(ss[:, i:i + 1], ss[:, i:i + 1], AF.Sqrt, reads=[f"ss{i}"], writes=[f"ss{i}"])
        P.recip(ss[:, i:i + 1], ss[:, i:i + 1], reads=[f"ss{i}"], writes=[f"ss{i}"])
        P.stt("dve", xt[j2][:], xt[j2][:], ss[:, i:i + 1], fnb[:], ALU.mult, ALU.mult,
              reads=[f"xt{j2}", f"ss{i}", "fnb"], writes=[f"xt{j2}"])
        P.dma("act", ov[i], xt[j2][:], reads=[f"xt{j2}"])


def inputs_D(inp, resE, x1_cores):
    c = make_consts()
    w_out = np.ascontiguousarray(inp["w_out_odd"][0])
    fnb = np.ascontiguousarray(np.broadcast_to(inp["final_norm"][None, :], (128, D)).astype(np.float32))
    gb = np.ascontiguousarray(np.broadcast_to(np.stack([inp["q_gain"][0], inp["k_gain"][0]])[None], (128, 2, 128)).astype(np.float32))
    maps = []
    for core in range(NCORES):
        b = core // 4
        kTf = np.ascontiguousarray(np.concatenate([resE[b * 4 + q]["kT"] for q in range(4)], axis=2))
        vxf = np.ascontiguousarray(np.concatenate([resE[b * 4 + q]["vx"] for q in range(4)], axis=0))
        maps.append(dict(qT=resE[core]["qT"], kTf=kTf, vxf=vxf, szT=resE[core]["szT"], x1=x1_cores[core],
                         w_out=w_out, fnb=fnb, gb=gb, ident=c["ident"]))
    return maps


_NC_CACHE = {}


def _get_nc(mode):
    if mode not in _NC_CACHE:
        _NC_CACHE[mode] = build(mode)
    return _NC_CACHE[mode]


def _run(mode, maps):
    return run_bass_kernel_spmd(_get_nc(mode), maps, core_ids=list(range(NCORES))).results


def kernel_unfused(x, norm_even, w_in_even, conv_w, w_out_even, norm_odd, w_in_odd, q_gain, k_gain, w_out_odd, final_norm):
    inp = dict(x=np.asarray(x, np.float32), norm_even=np.asarray(norm_even, np.float32),
               w_in_even=np.asarray(w_in_even, np.float32), conv_w=np.asarray(conv_w, np.float32),
               w_out_even=np.asarray(w_out_even, np.float32), norm_odd=np.asarray(norm_odd, np.float32),
               w_in_odd=np.asarray(w_in_odd, np.float32), q_gain=np.asarray(q_gain, np.float32),
               k_gain=np.asarray(k_gain, np.float32), w_out_odd=np.asarray(w_out_odd, np.float32),
               final_norm=np.asarray(final_norm, np.float32))
    resA = _run("A", inputs_A(inp))
    u_full = np.stack([np.concatenate([resA[b * 4 + q]["u"] for q in range(4)], 0) for b in range(2)])
    resB = _run("B", inputs_B(u_full))
    ft_full = np.stack([resB[c]["ft"] for c in range(NCORES)])
    resC = _run("C", inputs_C(inp, ft_full, [r["yaT"] for r in resA], [r["sbzT"] for r in resA]))
    x1c = [resC[c]["x1"] for c in range(NCORES)]
    resE = _run("E", inputs_E(inp, x1c))
    resD = _run("D", inputs_D(inp, resE, x1c))
    out = np.stack([np.concatenate([resD[b * 4 + q]["out"] for q in range(4)], 0) for b in range(2)])
    return out.astype(np.float32)


GROUPS = [[0, 1, 2, 3], [4, 5, 6, 7]]


class _Stop(Exception):
    pass


def build_fused():
    try:
        return _build_fused()
    except _Stop as e:
        return e.args[0]


def _build_fused():
    stop_at = int(os.environ.get("FUSED_STOP", "99"))
    nphase = [0]
    nc = bass.Bass("TRN2", target_bir_lowering=False)
    IN, OUT = "ExternalInput", "ExternalOutput"
    dr = lambda n, s, d, k=IN: nc.dram_tensor(n, s, d, kind=k).ap()
    x = dr("x", [T, D], F32)
    xh = dr("xh", [2, D], F32)
    g0 = dr("g0", [128, 8], F32)
    w_in0 = dr("w_in0", [D, 3072], F32)
    cw = dr("cw", [128, 12], F32)
    identd = dr("ident", [128, 128], BF16)
    c1s1d = dr("c1s1", [128, 256], BF16)
    w2d = dr("w2", [128, 128 * 128], BF16)
    ccscd = dr("ccsc", [128, 256], BF16)
    w_out0 = dr("w_out0", [D, D], F32)
    g1 = dr("g1", [128, 8], F32)
    w_in1 = dr("w_in1", [D, 2560], F32)
    gv = dr("gv", [128, 4], F32)
    cosd = dr("cos", [128, T], F32)
    sind = dr("sin", [128, T], F32)
    p0d = dr("p0", [128, 128], BF16)
    onesd = dr("ones", [128, 128], BF16)
    w_out1 = dr("w_out1", [D, D], F32)
    fnbd = dr("fnb", [128, D], F32)
    gbd = dr("gb", [128, 2, 128], F32)
    out = dr("out", [T, D], F32, OUT)
    gin_u = nc.dram_tensor("gin_u", [4 * T, 128], BF16).ap()
    gout_u = nc.dram_tensor("gout_u", [16 * T, 128], BF16).ap()
    gin_f = nc.dram_tensor("gin_f", [4 * 128, T], BF16).ap()
    gout_f = nc.dram_tensor("gout_f", [16 * 128, T], BF16).ap()
    gin_k = nc.dram_tensor("gin_k", [2 * 128, T], BF16).ap()
    gout_k = nc.dram_tensor("gout_k", [8 * 128, T], BF16).ap()
    gin_v = nc.dram_tensor("gin_v", [T, 258], BF16).ap()
    gout_v = nc.dram_tensor("gout_v", [4 * T, 258], BF16).ap()

    with ExitStack() as st0:
        P = Prog(nc)
        ccsems = [st0.enter_context(nc.semaphore(f"cc{i}")) for i in range(11)]
        sbR = lambda stk, n, s, d: stk.enter_context(nc.sbuf_tensor("s_" + n, s, d, side="right"))
        mk_sb = lambda stk: (lambda n, s, d: stk.enter_context(nc.sbuf_tensor("s_" + n, s, d)))
        mk_ps = lambda stk: (lambda n, s, d=F32: stk.enter_context(nc.psum_tensor("p_" + n, s, d)))
        xres = sbR(st0, "xres", [128, NT, D], F32)
        ident = st0.enter_context(nc.sbuf_tensor("s_ident", [128, 128], BF16))
        ccscr = st0.enter_context(nc.sbuf_tensor("s_ccscr", [128, 64], F32))
        P.dma("sp", ident[:], identd, writes=["ident"])
        P.nosig_scratch = ccscr[:, 16:64]
        P.nosig_n = 0
        xv = x.rearrange("(i p) d -> i p d", p=128)

        def emit_phase():
            P.emit(st0)
            nphase[0] += 1
            if nphase[0] >= stop_at:
                raise _Stop(nc)

        def ag_start(i, src, dst, reads):
            sem = ccsems[i]
            P.op("pool", lambda e: e.collective_compute("AllGather", ALU.bypass, replica_groups=GROUPS,
                                                        ins=[src.opt()], outs=[dst.opt()]).then_inc(sem),
                 reads, [f"cc_inflight{i}"], nosignal=True)

        def ag_wait(i, writes):
            sem = ccsems[i]

            def fn(e):
                e.wait_ge(sem, 1)
                return e.memset(ccscr[:, i:i + 1], 0.0)
            P.op("pool", fn, [f"cc_inflight{i}"], writes)

        with ExitStack() as stL0:
            yaT = sbR(stL0, "yaT", [128, 4, T], BF16)
            sbzT = sbR(stL0, "sbzT", [128, 4, T], BF16)
            with ExitStack() as st:
                sb, ps = mk_sb(st), mk_ps(st)
                g0s = sb("g0s", [128, 8], F32)
                cws = sb("cws", [128, 12], F32)
                gfull = sb("gfull", [128, 8, 128], F32)
                hT = sb("hT", [128, 8, T], BF16)
                hTh = sb("hTh", [128, 8, 128], BF16)
                st1 = ExitStack()
                sb1 = mk_sb(st1)
                xht = sb1("xht", [128, D], F32)
                P.dma("sp", g0s[:], g0, writes=["g0s"])
                P.dma("sp", cws[:], cw, writes=["cws"])
                P.memset("pool", gfull[:], 1.0, writes=["gfull"])
                for k in range(8):
                    P.tsmul("pool", gfull[:, k, :], gfull[:, k, :], g0s[:, k:k + 1], reads=["g0s", "gfull"], writes=["gfull"])

                def loadA(i):
                    if i < NT:
                        P.dma("sp", xres[:, i, :], xv[i], writes=[f"xres{i}"])
                    else:
                        P.memset("pool", xht[:], 0.0, writes=["xht"])
                        P.dma("sp", xht[0:2, :], xh, writes=["xht"])

                pp = [ps(f"ppA{j}", [128, 512]) for j in range(4)]
                ppi = [0]
                wu_all = sb1("wu_all", [128, 8, 512], BF16)
                ustg = [sb1(f"ustg{j}", [128, 8, 128], F32) for j in range(2)]
                w0v = w_in0.rearrange("(k p) n -> p k n", p=128)
                for g_ in range(4):
                    P.dma("sp", ustg[g_ % 2][:], w0v[:, :, 2048 + g_ * 128:2048 + (g_ + 1) * 128], writes=[f"ustg{g_ % 2}"])
                    P.tt("pool", wu_all[:, :, g_ * 128:(g_ + 1) * 128], ustg[g_ % 2][:], gfull[:], ALU.mult,
                         reads=[f"ustg{g_ % 2}", "gfull"], writes=[f"wu{g_}"])
                ubt = [sb1(f"ubt{j}", [128, 512], BF16) for j in range(2)]
                ginu_t = gin_u.rearrange("(g i p) c -> i p g c", g=4, p=128)

                def u_tile(i):
                    if i >= NT:
                        return
                    pj = ppi[0] % 4
                    ppi[0] += 1
                    for k in range(8):
                        P.mm(pp[pj][:], hT[:, k, i * 128:(i + 1) * 128], wu_all[:, k, :], k == 0, k == 7,
                             reads=[f"wu{g_}" for g_ in range(4)] + [f"hT{i // 4}"], writes=[f"pp{pj}"])
                    P.copy("dve", ubt[i % 2][:], pp[pj][:], reads=[f"pp{pj}"], writes=[f"ubt{i % 2}"])
                    P.dma("sp", ginu_t[i], ubt[i % 2][:].rearrange("p (g c) -> p g c", g=4), reads=[f"ubt{i % 2}"], writes=[f"gin_u_t{i}"])

                for i in range(NT + 1):
                    loadA(i)
                rmsnorm_to_hT(P, nc, st1, lambda i: (xres[:, i, :] if i < NT else xht[:]),
                              lambda i: (f"xres{i}" if i < NT else "xht"), NT + 1,
                              lambda i: (hT[:, :, i * 128:(i + 1) * 128] if i < NT else hTh[:, :, :]),
                              lambda i: (f"hT{i // 4}" if i < NT else "hTh"), ident, "rnA", load_fn=None, after_tile=u_tile)
                emit_phase()
                st1.close()
                order = []
                for j in range(4):
                    order += [j, 8 + j, 4 + j, 12 + j]
                order += [20, 21, 22, 23]
                ph = ps("phA", [128, 2, 2])
                tbuf = sb("tbuf", [128, T + 2], F32)
                cbuf = sb("cbuf", [128, T], F32)
                axs = [sb(f"axs{j}", [128, 512], F32) for j in range(2)]
                szs = [sb(f"szs{j}", [128, 512], F32) for j in range(2)]
                vs = [sb(f"vs{j}", [128, 512], F32) for j in range(2)]
                hprod = sb("hprod", [128, 2, 2], F32)

                def projA(wb, wtok, tb):
                    j = ppi[0] % 4
                    ppi[0] += 1
                    for k in range(8):
                        P.mm(pp[j][:], wb[:, k, :], hT[:, k, tb * 512:(tb + 1) * 512], k == 0, k == 7,
                             reads=[wtok, f"hT{tb}"], writes=[f"pp{j}"])
                    return pp[j], f"pp{j}"

                chunks = {}
                NBA = 16
                genA = stream_w_chunks(P, nc, sb, w_in0, gfull, 3072, order, "wiA", nstage=2, nbuf=NBA, engs=("dve",), eager=True, ahead=NBA - 2)
                firstA = next(genA)
                for g_ in range(4):
                    ag_start(g_, gin_u[g_ * T:(g_ + 1) * T, :], gout_u[g_ * 4 * T:(g_ + 1) * 4 * T, :],
                             [f"gin_u_t{i}" for i in range(NT)] + [f"wiA_wb{jj}" for jj in range(NBA - 1)])

                def _chainA():
                    yield firstA
                    for it in genA:
                        yield it
                for cid, wb, wtok in _chainA():
                    kind, j = cid // 4, cid % 4
                    if kind == 4:
                        for i in range(NT):
                            pj = ppi[0] % 4
                            ppi[0] += 1
                            for k in range(8):
                                P.mm(pp[pj][:, 0:128], hT[:, k, i * 128:(i + 1) * 128], wb[:, k, :], k == 0, k == 7,
                                     reads=[wtok, f"hT{i // 4}"], writes=[f"pp{pj}"])
                            P.copy("act", ub[j % 2][:, i, :], pp[pj][:, 0:128], reads=[f"pp{pj}"], writes=[f"ub{j % 2}"])
                        P.dma("act", ginu_v[j], ub[j % 2][:], reads=[f"ub{j % 2}"], writes=[f"gin_u{j}"])
                        ag_start(j, gin_u[j * T:(j + 1) * T, :], gout_u[j * 4 * T:(j + 1) * 4 * T, :], [f"gin_u{j}"])
                    elif kind == 5:
                        for tb in range(4):
                            pt, ptok = projA(wb, wtok, tb)
                            P.act(sbzT[:, j, tb * 512:(tb + 1) * 512], pt[:], AF.Silu, reads=[ptok], writes=[f"sbzT{j}"])
                    elif kind == 0:
                        chunks["x"] = (wb, wtok)
                    elif kind == 2:
                        wbx, wtokx = chunks["x"]
                        for tb in range(4):
                            ptx, ptokx = projA(wbx, wtokx, tb)
                            a = tb % 2
                            P.copy("act", axs[a][:], ptx[:], reads=[ptokx], writes=[f"axs{a}"])
                            ptc, ptokc = projA(wb, wtok, tb)
                            P.tt("dve", tbuf[:, 1 + tb * 512:1 + (tb + 1) * 512], ptc[:], axs[a][:], ALU.mult,
                                 reads=[ptokc, f"axs{a}"], writes=[f"tbuf{tb}"])
                        for k in range(8):
                            P.mm(ph[:, 0, :], wbx[:, k, :], hTh[:, k, 0:2], k == 0, k == 7, reads=[wtokx, "hTh"], writes=["phb"])
                        for k in range(8):
                            P.mm(ph[:, 1, :], wb[:, k, :], hTh[:, k, 0:2], k == 0, k == 7, reads=[wtok, "hTh"], writes=["phb"])
                        P.copy("act", hprod[:, 0, :], ph[:, 0, :], reads=["phb"], writes=["hprod0"])
                        P.tt("dve", hprod[:, 1, :], ph[:, 1, :], hprod[:, 0, :], ALU.mult, reads=["phb", "hprod0"], writes=["hprod1"])
                        P.copy("dve", tbuf[:, 0:1], hprod[:, 1, 0:1], reads=["hprod1"], writes=["tbufL"])
                        P.copy("dve", tbuf[:, T + 1:T + 2], hprod[:, 1, 1:2], reads=["hprod1"], writes=["tbufR"])
                        alltb = [f"tbuf{tb}" for tb in range(4)]
                        P.act(cbuf[:], tbuf[:, 1:T + 1], AF.Copy, reads=alltb, writes=["cbuf"], scale=cws[:, 3 * j + 1:3 * j + 2])
                        P.stt("dve", cbuf[:], tbuf[:, 0:T], cws[:, 3 * j:3 * j + 1], cbuf[:], ALU.mult, ALU.add,
                              reads=alltb + ["tbufL", "cws"], writes=["cbuf"])
                        P.stt("dve", cbuf[:], tbuf[:, 2:T + 2], cws[:, 3 * j + 2:3 * j + 3], cbuf[:], ALU.mult, ALU.add,
                              reads=alltb + ["tbufR", "cws"], writes=["cbuf"])
                    elif kind == 1:
                        chunks["b"] = (wb, wtok)
                    elif kind == 3:
                        wbb, wtokb = chunks["b"]
                        for tb in range(4):
                            a = tb % 2
                            ptz, ptokz = projA(wb, wtok, tb)
                            P.act(szs[a][:], ptz[:], AF.Silu, reads=[ptokz], writes=[f"szs{a}"])
                            ptb, ptokb = projA(wbb, wtokb, tb)
                            P.tt("dve", vs[a][:], ptb[:], szs[a][:], ALU.mult, reads=[ptokb, f"szs{a}"], writes=[f"vs{a}"])
                            P.tt("dve", yaT[:, j, tb * 512:(tb + 1) * 512], cbuf[:, tb * 512:(tb + 1) * 512], vs[a][:], ALU.mult,
                                 reads=["cbuf", f"vs{a}"], writes=[f"yaT{j}"])
                for g_ in range(4):
                    ag_wait(g_, [f"gout_u{g_}"])
                emit_phase()
            with ExitStack() as stB:
                X1 = mk_sb(stB)("X1", [128, 128, 128], BF16)
                WoA = mk_sb(stB)("WoA", [128, 4, D], BF16)
                W2 = mk_sb(stB)("W2", [128, 128, 128], BF16)
                with ExitStack() as st:
                    sb, ps = mk_sb(st), mk_ps(st)
                    X0 = sb("X0", [128, 2, 64, 128], BF16)
                    c1s1 = sb("c1s1", [128, 256], BF16)
                    pb = [ps(f"pbA{j}", [128, 512]) for j in range(4)]
                    P.dma("sp", c1s1[:], c1s1d, writes=["c1s1"])
                    def x0_load(r, q):
                        def fn(e):
                            rank = P.rank4(e)
                            src = gout_u.rearrange("(g q t) c -> g q t c", q=4, g=4)[bass.ds(rank, 1)]
                            src = src.rearrange("o q (p j) c -> (o q) p j c", j=64)[q]
                            return e.dma_start(out=X0[q * 32:(q + 1) * 32, r, :, :], in_=src)
                        return fn
                    for q in range(4):
                        P.op("pool", x0_load(0, q), [f"gout_u{g_}" for g_ in range(4)], [f"X0a{q}"], dma=True,
                             own_sem=st0.enter_context(nc.semaphore(f"dyn_x0a{q}")))
                        P.op("pool", x0_load(1, q), [f"gout_u{g_}" for g_ in range(4)], [f"X0b{q}"], dma=True,
                             own_sem=st0.enter_context(nc.semaphore(f"dyn_x0b{q}")))
                    wstA = [sb(f"wstA{j}", [128, 512], F32) for j in range(3)]
                    nwA = 0
                    for nb in range(2):
                        for kc in range(4):
                            P.dma("sp", wstA[nwA % 3][:], w_out0[kc * 128:(kc + 1) * 128, nb * 512:(nb + 1) * 512], reads=[f"X0a{q}" for q in range(4)] + [f"X0b{q}" for q in range(4)], writes=[f"wstA{nwA % 3}"])
                            P.copy("pool", WoA[:, kc, nb * 512:(nb + 1) * 512], wstA[nwA % 3][:],
                                   reads=[f"wstA{nwA % 3}"], writes=[f"Wo0_{kc}_{nb}"])
                            nwA += 1
                    w2v = w2d.rearrange("p (a b) -> p a b", b=128)
                    for h in range(4):
                        P.dma("sp", W2[:, h * 32:(h + 1) * 32, :], w2v[:, h * 32:(h + 1) * 32, :],
                              reads=[f"X0a{q}" for q in range(4)] + [f"X0b{q}" for q in range(4)], writes=[f"W2_{h}"])

                    n = 0
                    for c0 in range(0, 128, 2):
                        j = n % 4
                        n += 1
                        pt = pb[j][:].rearrange("p (a b) -> p a b", b=256)
                        for cc in range(2):
                            P.mm(pt[:, cc, :], X0[:, :, :, c0 + cc], c1s1[:], True, True, reads=[f"X0a{q}" for q in range(4)] + [f"X0b{q}" for q in range(4)] + ["c1s1"], writes=[f"pb{j}"])
                        e1, e2 = ("dve", "act") if (c0 // 2) % 2 == 0 else ("act", "dve")
                        P.copy(e1, X1[0:64, c0:c0 + 2, :], pt[0:64, :, 0:128], reads=[f"pb{j}"], writes=[f"X1r{c0}", f"lkb{j}"])
                        P.copy(e1, X1[64:128, c0:c0 + 2, :], pt[64:128, :, 128:256], reads=[f"pb{j}", f"lkb{j}"], writes=[f"X1i{c0}"])
                    emit_phase()
                with ExitStack() as st:
                    sb, ps = mk_sb(st), mk_ps(st)
                    X2 = sb("X2", [128, 2, S], BF16)
                    ccsc = sb("ccsc", [128, 256], BF16)
                    fo = [sb(f"fo{j}", [128, 1024], BF16) for j in range(2)]
                    pb = [ps(f"pbB{j}", [128, 512]) for j in range(4)]
                    X2r, X2i = X2[:, 0, :], X2[:, 1, :]
                    P.dma("sp", ccsc[:], ccscd, writes=["ccsc"])
                    n = 0
                    for k0 in range(0, 128, 4):
                        j = n % 4
                        n += 1
                        pt = pb[j][:].rearrange("p (a b) -> p a b", b=128)
                        for kk in range(4):
                            k1 = k0 + kk
                            P.mm(pt[:, kk, :], X1[:, :, k1], W2[:, k1, :], True, True, reads=[f"W2_{k1 // 32}"], writes=[f"pb{j}"])
                        X2v = X2[:].rearrange("p r (k1 k2) -> p k1 r k2", k2=64)[:, k0:k0 + 4, :, :]
                        P.copy("dve" if (k0 // 4) % 2 == 0 else "act", X2v, pb[j][:].rearrange("p (a r b) -> p a r b", r=2, b=64),
                               reads=[f"pb{j}"], writes=[f"X2_{k0}"])
                    x2all = [f"X2_{k0}" for k0 in range(0, 128, 4)]
                    ginf_v = gin_f.rearrange("(q l) t -> q l t", q=4)
                    for kb in range(16):
                        j = n % 4
                        n += 1
                        rr_ = X2r.rearrange("p (k1 k2) -> p k2 k1", k2=64)[:, 4 * kb:4 * kb + 4, :]
                        ri_ = X2i.rearrange("p (k1 k2) -> p k2 k1", k2=64)[:, 4 * kb:4 * kb + 4, :]
                        P.mm(pb[j][:], ccsc[:, 0:128], rr_, True, False, reads=x2all + ["ccsc"], writes=[f"pb{j}"])
                        P.mm(pb[j][:], ccsc[:, 128:256], ri_, False, True, reads=x2all + ["ccsc"], writes=[f"pb{j}"])
                        f = kb // 2
                        P.copy("dve" if kb % 2 == 0 else "act", fo[f % 2][:, (kb % 2) * 512:(kb % 2 + 1) * 512], pb[j][:],
                               reads=[f"pb{j}"], writes=[f"fo{f % 2}_{kb % 2}"])
                        if kb % 2 == 1:
                            P.dma("sp", ginf_v[f // 2][:, (f % 2) * 1024:(f % 2 + 1) * 1024], fo[f % 2][:],
                                  reads=[f"fo{f % 2}_0", f"fo{f % 2}_1"], writes=[f"gin_f{f}"])
                            if f % 2 == 1:
                                q_ = f // 2
                                ag_start(4 + q_, gin_f[q_ * 128:(q_ + 1) * 128, :], gout_f[q_ * 512:(q_ + 1) * 512, :],
                                         [f"gin_f{2 * q_}", f"gin_f{2 * q_ + 1}"])
                    emit_phase()
                with ExitStack() as st:
                    sb, ps = mk_sb(st), mk_ps(st)
                    ybT = sb("ybT", [128, 4, T], BF16)
                    fst = [sb(f"fst{j}", [128, T], BF16) for j in range(2)]
                    WoB = sb("WoB", [128, 4, D], BF16)
                    wst = [sb(f"wstC{j}", [128, 512], F32) for j in range(3)]
                    pso = [ps(f"psoC{j}", [128, 512]) for j in range(4)]

                    def ft_load(g, dst):
                        def fn(e):
                            rank = P.rank4(e)
                            src = gout_f.rearrange("(q g l) t -> q g l t", g=4, q=4)[bass.ds(rank, 1)]
                            src = src.rearrange("o g l t -> (o g) l t")[g]
                            return e.dma_start(out=dst, in_=src)
                        return fn
                    n = 0
                    for nb in range(2):
                        for i in range(NT):
                            j = n % 4
                            n += 1
                            for kc in range(4):
                                P.mm(pso[j][:], yaT[:, kc, i * 128:(i + 1) * 128], WoA[:, kc, nb * 512:(nb + 1) * 512], kc == 0, kc == 3,
                                     reads=[f"yaT{kc}", f"Wo0_{kc}_{nb}"], writes=[f"pso{j}"])
                            P.tt("dve", xres[:, i, nb * 512:(nb + 1) * 512], pso[j][:], xres[:, i, nb * 512:(nb + 1) * 512], ALU.add,
                                 reads=[f"pso{j}", f"xres{i}"], writes=[f"xres{i}"])
                    nw = 0
                    for nb in range(2):
                        for kc in range(4):
                            P.dma("sp", wst[nw % 3][:], w_out0[(4 + kc) * 128:(5 + kc) * 128, nb * 512:(nb + 1) * 512], writes=[f"wst{nw % 3}"])
                            P.copy("act", WoB[:, kc, nb * 512:(nb + 1) * 512], wst[nw % 3][:],
                                   reads=[f"wst{nw % 3}"], writes=[f"Wo0_{4 + kc}_{nb}"])
                            nw += 1
                    for q_ in range(4):
                        ag_wait(4 + q_, [f"gout_f{q_}"])
                    for g in range(4):
                        P.op("pool", ft_load(g, fst[g % 2][:]), [f"gout_f{q_}" for q_ in range(4)], [f"fst{g % 2}"], dma=True,
                             own_sem=st0.enter_context(nc.semaphore(f"dyn_ft{g}")))
                        P.tt("dve", ybT[:, g, :], fst[g % 2][:], sbzT[:, g, :], ALU.mult,
                             reads=[f"fst{g % 2}", f"sbzT{g}"], writes=[f"ybT{g}"])
                    for nb in range(2):
                        for i in range(NT):
                            j = n % 4
                            n += 1
                            for kc in range(4):
                                P.mm(pso[j][:], ybT[:, kc, i * 128:(i + 1) * 128], WoB[:, kc, nb * 512:(nb + 1) * 512], kc == 0, kc == 3,
                                     reads=[f"ybT{kc}", f"Wo0_{4 + kc}_{nb}"], writes=[f"pso{j}"])
                            P.tt("dve", xres[:, i, nb * 512:(nb + 1) * 512], pso[j][:], xres[:, i, nb * 512:(nb + 1) * 512], ALU.add,
                                 reads=[f"pso{j}", f"xres{i}"], writes=[f"xres{i}"])
                    emit_phase()
        with ExitStack() as stL1:
            qT = sbR(stL1, "qT", [128, 8, T], BF16)
            with ExitStack() as stE:
                sbE = mk_sb(stE)
                hT = sbE("hT1", [128, 8, T], BF16)
                g1s = sbE("g1s", [128, 8], F32)
                gfull = sbE("gfull1", [128, 8, 128], F32)
                with ExitStack() as st:
                    sb, ps = mk_sb(st), mk_ps(st)
                    p0 = sb("p0", [128, 128], BF16)
                    ones = sb("ones", [128, 128], BF16)
                    gvs = sb("gvs", [128, 4], F32)
                    cos = sb("cos", [128, T], F32)
                    sin = sb("sin", [128, T], F32)
                    for t_, d_, n_ in ((p0, p0d, "p0"), (ones, onesd, "ones"), (g1s, g1, "g1s"), (gvs, gv, "gvs"),
                                       (cos, cosd, "cos"), (sin, sind, "sin")):
                        P.dma("sp", t_[:], d_, writes=[n_])
                    P.memset("pool", gfull[:], 1.0, writes=["gfull"])
                    for k in range(8):
                        P.tsmul("pool", gfull[:, k, :], gfull[:, k, :], g1s[:, k:k + 1], reads=["g1s", "gfull"], writes=["gfull"])
                    stR = ExitStack()
                    rmsnorm_to_hT(P, nc, stR, lambda i: xres[:, i, :], lambda i: f"xres{i}", NT,
                                  lambda i: hT[:, :, i * 128:(i + 1) * 128], lambda i: f"hT{i // 4}", ident, "rnE")
                    emit_phase()
                    stR.close()
                    pp = [ps(f"ppE{j}", [128, 512]) for j in range(4)]
                    pss = ps("pss", [128, 512])
                    prot = ps("prot", [128, 512])
                    qb = [sb(f"qb{j}", [128, 512], BF16) for j in range(3)]
                    sq = [sb(f"sq{j}", [128, 512], BF16) for j in range(3)]
                    rs = [sb(f"rs{j}", [128, 512], F32) for j in range(2)]
                    ta = [sb(f"ta{j}", [128, 512], F32) for j in range(2)]
                    tb_ = [sb(f"tb{j}", [128, 512], F32) for j in range(2)]
                    kbuf = sb("kbuf", [128, 2, T], BF16)
                    vbuf = sb("vbuf", [128, NT, 129], BF16)
                    ppi = [0]
                    cnt = [0]

                    def projE(wb, wtok, tb):
                        j = ppi[0] % 4
                        ppi[0] += 1
                        for k in range(8):
                            P.mm(pp[j][:], wb[:, k, :], hT[:, k, tb * 512:(tb + 1) * 512], k == 0, k == 7,
                                 reads=[wtok, f"hT{tb}"], writes=[f"pp{j}"])
                        return pp[j], f"pp{j}"

                    gink_v = gin_k.rearrange("(g d) t -> g d t", g=2)
                    ginv_v = gin_v.rearrange("(i p) (a b) -> a p i b", p=128, a=2)
                    epsT = sb("epsT", [128, 1], F32)
                    P.memset("dve", epsT[:], EPS, writes=["epsT"])
                    order = [8, 9, 10, 11] + list(range(8))
                    pend = []

                    def stage1(cid, wb, wtok, tb):
                        a = cnt[0] % 3
                        cnt[0] += 1
                        pt, ptok = projE(wb, wtok, tb)
                        P.copy("act", qb[a][:], pt[:], reads=[ptok], writes=[f"qb{a}", f"lk1_{ptok}"])
                        P.act(sq[a][:], pt[:], AF.Square, reads=[ptok], writes=[f"sq{a}", f"lk2_{ptok}"])
                        return (cid, tb, a, pt, ptok)

                    s2cnt = [0]

                    def stage2(item):
                        cid, tb, a, pt, ptok = item
                        b = s2cnt[0] % 2
                        s2cnt[0] += 1
                        isq = cid < 8
                        gcol = 0 if isq else 2
                        sl = slice(tb * 512, (tb + 1) * 512)
                        P.mm(pss[:], ones[:], sq[a][:], True, True, reads=["ones", f"sq{a}"], writes=["pss"])
                        P.mm(prot[:], p0[:], qb[a][:], True, True, reads=["p0", f"qb{a}"], writes=["prot"])
                        P.act(rs[b][:], pss[:], AF.Ln, reads=["pss", "epsT"], writes=[f"rs{b}"], scale=1.0 / 128, bias=epsT[:, 0:1])
                        P.act(rs[b][:], rs[b][:], AF.Exp, reads=[f"rs{b}"], writes=[f"rs{b}"], scale=-0.5)
                        P.stt("dve", ta[b][:], pt[:], gvs[:, gcol:gcol + 1], cos[:, sl], ALU.mult, ALU.mult,
                              reads=[ptok, f"lk1_{ptok}", f"lk2_{ptok}", "gvs", "cos"], writes=[f"ta{b}"])
                        P.stt("dve", tb_[b][:], prot[:], gvs[:, gcol + 1:gcol + 2], sin[:, sl], ALU.mult, ALU.mult,
                              reads=["prot", "gvs", "sin"], writes=[f"tb{b}"])
                        P.tt("dve", ta[b][:], ta[b][:], tb_[b][:], ALU.add, reads=[f"ta{b}", f"tb{b}"], writes=[f"ta{b}"])
                        if isq:
                            P.tt("dve", qT[:, cid, sl], ta[b][:], rs[b][:], ALU.mult, reads=[f"ta{b}", f"rs{b}"], writes=[f"q{cid}_{tb}"])
                        else:
                            P.tt("dve", kbuf[:, cid - 8, sl], ta[b][:], rs[b][:], ALU.mult, reads=[f"ta{b}", f"rs{b}"], writes=[f"kbuf{cid - 8}"])
                            if tb == 3:
                                P.dma("sp", gink_v[cid - 8], kbuf[:, cid - 8, :], reads=[f"kbuf{cid - 8}"], writes=[f"gin_k{cid - 8}"])
                                if cid == 9:
                                    ag_start(8, gin_k, gout_k, ["gin_k0", "gin_k1"])

                    for cid, wb, wtok in stream_w_chunks(P, nc, sb, w_in1, gfull, 2560, order, "wiE", nstage=2, nbuf=10, engs=("dve",), eager=True):
                        if cid < 10:
                            for tb in range(4):
                                item = stage1(cid, wb, wtok, tb)
                                if len(pend) >= 2:
                                    stage2(pend.pop(0))
                                pend.append(item)
                        else:
                            while pend:
                                stage2(pend.pop(0))
                            kvh = cid - 10
                            P.memset("pool", vbuf[:, :, 128:129], 1.0, writes=["vbuf"])
                            for i in range(NT):
                                j = ppi[0] % 4
                                ppi[0] += 1
                                for k in range(8):
                                    P.mm(pp[j][:, 0:128], hT[:, k, i * 128:(i + 1) * 128], wb[:, k, :], k == 0, k == 7,
                                         reads=[wtok, f"hT{i // 4}"], writes=[f"pp{j}"])
                                P.copy("act", vbuf[:, i, 0:128], pp[j][:, 0:128], reads=[f"pp{j}", "vbuf"], writes=[f"vbuf_{i}"])
                            P.dma("sp", ginv_v[kvh], vbuf[:], reads=[f"vbuf_{i}" for i in range(NT)] + ["vbuf"], writes=["vbuf", f"gin_v{kvh}"])
                            if kvh == 1:
                                for h_ in range(2):
                                    ag_start(9 + h_, gin_v[h_ * 1024:(h_ + 1) * 1024, :], gout_v[h_ * 4096:(h_ + 1) * 4096, :],
                                             ["gin_v0", "gin_v1"])
                    while pend:
                        stage2(pend.pop(0))
                    ag_wait(8, ["gout_k"])
                    for h_ in range(2):
                        ag_wait(9 + h_, [f"gout_v{h_}"])
                    emit_phase()
                szT = sbR(stL1, "szT", [128, 8, T], BF16)
                with ExitStack() as st:
                    sb, ps = mk_sb(st), mk_ps(st)
                    pp = [ps(f"ppZ{j}", [128, 512]) for j in range(4)]
                    n = 0
                    for cid, wb, wtok in stream_w_chunks(P, nc, sb, w_in1, gfull, 2560, list(range(12, 20)), "wiZ"):
                        for tb in range(4):
                            j = n % 4
                            n += 1
                            for k in range(8):
                                P.mm(pp[j][:], wb[:, k, :], hT[:, k, tb * 512:(tb + 1) * 512], k == 0, k == 7,
                                     reads=[wtok, f"hT{tb}"], writes=[f"pp{j}"])
                            P.act(szT[:, cid - 12, tb * 512:(tb + 1) * 512], pp[j][:], AF.Silu, reads=[f"pp{j}"], writes=[f"sz{cid - 12}"])
                    emit_phase()
            with ExitStack() as st:
                sb, ps = mk_sb(st), mk_ps(st)
                gb = sb("gb", [128, 2, 128], F32)
                mx = sb("mx", [128, 4], F32)
                kT = sb("kT", [128, 2, S], BF16)
                V = sb("V", [128, 64, 258], BF16)
                PT = [sb(f"PT{j}", [128, 2, 512], BF16) for j in range(4)]
                on = [sb(f"on{j}", [128, 128], BF16) for j in range(4)]
                rden = sb("rden", [128, 8], F32)
                oc = [sb(f"oc{j}", [128, 2, 129], F32) for j in range(2)]
                pS = [ps(f"pS{j}", [128, 2, 512]) for j in range(3)]
                pO = [ps(f"pO{j}", [128, 2, 129]) for j in range(2)]
                slot = [0]
                sslot = {}
                P.dma("sp", gb[:], gbd, writes=["gb"])
                gk_v = gout_k.rearrange("(q g d) t -> g d q t", q=4, g=2)
                Vd = V[:].rearrange("p (q h c) f -> p q h c f", q=4, h=2)
                vsrc = [gout_v[h_ * 4096:(h_ + 1) * 4096, :].rearrange("(q c p) f -> p q c f", q=4, p=128) for h_ in range(2)]
                for q_ in range(4):
                    P.dma("sp", kT[:, 0, q_ * T:(q_ + 1) * T], gk_v[0][:, q_, :], reads=["gout_k"], writes=[f"kT0_{q_}"])
                    for h_ in range(2):
                        P.dma("sp", Vd[:, q_, h_, :, :], vsrc[h_][:, q_, :, :], reads=[f"gout_v{h_}"], writes=[f"V{q_}_{h_}"])
                for q_ in range(4):
                    P.dma("sp", kT[:, 1, q_ * T:(q_ + 1) * T], gk_v[1][:, q_, :], reads=["gout_k"], writes=[f"kT1_{q_}"])
                scale = 128.0 ** -0.5
                P.op("dve", lambda e: e.reduce_max(mx[:, 0:1], gb[:, 0, :], AX.X, apply_absolute_value=True), reads=["gb"], writes=["mx0"])
                P.op("dve", lambda e: e.reduce_max(mx[:, 1:2], gb[:, 1, :], AX.X, apply_absolute_value=True), reads=["gb"], writes=["mx1"])
                P.tt("dve", mx[:, 2:3], mx[:, 0:1], mx[:, 1:2], ALU.mult, reads=["mx0", "mx1"], writes=["mx2"])
                P.ts("dve", mx[:, 3:4], mx[:, 2:3], -scale * 128.0, 0.0, ALU.mult, ALU.add, reads=["mx2"], writes=["nbias"])
                Vv = V[:].rearrange("p c (g f) -> p c g f", g=2)
                iters = [(h, qb_, kc2) for h in range(8) for qb_ in range(4) for kc2 in range(32)]
                NI = len(iters)

                def emit_S(n):
                    h, qb_, kc2 = iters[n]
                    g = h // 4
                    j = slot[0] % 3
                    slot[0] += 1
                    jp = n % 4
                    qsl = slice(qb_ * 512, (qb_ + 1) * 512)
                    for c2 in range(2):
                        kc = 2 * kc2 + c2
                        P.mm(pS[j][:, c2, :], kT[:, g, kc * 128:(kc + 1) * 128], qT[:, h, qsl], True, True,
                             reads=[f"kT{g}_{kc // 16}", f"q{h}_{qb_}"], writes=[f"pS{j}"])
                    P.act(PT[jp][:], pS[j][:], AF.Exp, reads=[f"pS{j}", "nbias"], writes=[f"PT{jp}"], scale=scale, bias=mx[:, 3:4])

                def emit_PV(n):
                    h, qb_, kc2 = iters[n]
                    g = h // 4
                    jp = n % 4
                    qsl = slice(qb_ * 512, (qb_ + 1) * 512)
                    for c2 in range(2):
                        kc = 2 * kc2 + c2
                        for qt in range(4):
                            P.mm(pO[qt // 2][:, qt % 2, :], PT[jp][:, c2, qt * 128:(qt + 1) * 128], Vv[:, kc, g, :],
                                 kc == 0, kc == 63, reads=[f"PT{jp}", f"V{kc // 16}_{(kc % 16) // 8}"], writes=[f"pOb{qt // 2}"])
                    if kc2 == 31:
                        for b_ in range(2):
                            P.copy("dve", oc[b_][:], pO[b_][:], reads=[f"pOb{b_}"], writes=[f"oc{b_}"])
                        for qt in range(4):
                            acc = oc[qt // 2][:, qt % 2, :]
                            col = (h * 4 + qb_) % 2 * 4 + qt
                            P.recip(rden[:, col:col + 1], acc[:, 128:129], reads=[f"oc{qt // 2}"], writes=[f"rden{col}"])
                            P.tsmul("dve", on[qt][:], acc[:, 0:128], rden[:, col:col + 1], reads=[f"oc{qt // 2}", f"rden{col}"], writes=[f"on{qt}"])
                        deferred.append((h, qb_))

                def emit_fin(h, qb_):
                    qsl = slice(qb_ * 512, (qb_ + 1) * 512)
                    j = slot[0] % 3
                    slot[0] += 1
                    pTv = pS[j][:, 0, 0:256].bitcast(BF16)
                    for qt in range(4):
                        P.tr(pTv[:, qt * 128:(qt + 1) * 128], on[qt][:], ident[:], reads=[f"on{qt}", "ident"], writes=[f"pS{j}"])
                    P.tt("dve", qT[:, h, qsl], pTv, szT[:, h, qsl], ALU.mult,
                         reads=[f"pS{j}", f"sz{h}"], writes=[f"q{h}_{qb_}"])

                deferred = []
                LOOK = 2
                for n in range(NI + LOOK):
                    if n < NI:
                        emit_S(n)
                    if n >= LOOK:
                        m = n - LOOK
                        emit_PV(m)
                        if len(deferred) and iters[m][2] == 0:
                            had = list(deferred)
                            del deferred[:]
                            for (h_, q__) in had:
                                emit_fin(h_, q__)
                for (h_, q__) in deferred:
                    emit_fin(h_, q__)
                emit_phase()
            with ExitStack() as st:
                sb, ps = mk_sb(st), mk_ps(st)
                Wo = sb("Wo1", [128, 8, D], BF16)
                wst = [sb(f"wstD{j}", [128, 512], F32) for j in range(3)]
                fnb = sb("fnb", [128, D], F32)
                junk = sb("junkD", [128, D], BF16)
                ss = sb("ssD", [128, NT], F32)
                epsD = sb("epsD", [128, 1], F32)
                pso = [ps(f"psoD{j}", [128, 512]) for j in range(4)]
                nw = 0
                for nb in range(2):
                    for kc in range(8):
                        P.dma("sp", wst[nw % 3][:], w_out1[kc * 128:(kc + 1) * 128, nb * 512:(nb + 1) * 512], writes=[f"wst{nw % 3}"])
                        P.copy(("dve", "pool", "act")[nw % 3], Wo[:, kc, nb * 512:(nb + 1) * 512], wst[nw % 3][:],
                               reads=[f"wst{nw % 3}"], writes=[f"Wo1_{kc}_{nb}"])
                        nw += 1
                P.dma("sp", fnb[:], fnbd, writes=["fnb"])
                ov = out.rearrange("(i p) d -> i p d", p=128)
                P.memset("dve", ss[:], 0.0, writes=["ss"])
                P.memset("dve", epsD[:], EPS, writes=["epsD"])
                n = 0
                for nb in range(2):
                    for i in range(NT):
                        j = n % 4
                        n += 1
                        for h in range(8):
                            P.mm(pso[j][:], qT[:, h, i * 128:(i + 1) * 128], Wo[:, h, nb * 512:(nb + 1) * 512], h == 0, h == 7,
                                 reads=[f"q{h}_{i // 4}", f"Wo1_{h}_{nb}"], writes=[f"pso{j}"])
                        P.tt("dve", xres[:, i, nb * 512:(nb + 1) * 512], pso[j][:], xres[:, i, nb * 512:(nb + 1) * 512], ALU.add,
                             reads=[f"pso{j}", f"xres{i}"], writes=[f"xres{i}"])
                        if nb == 1:
                            P.act(junk[:], xres[:, i, :], AF.Square, reads=[f"xres{i}", "ss"], writes=["junk", f"ss{i}"], accum_out=ss[:, i:i + 1])
                            P.act(ss[:, i:i + 1], ss[:, i:i + 1], AF.Ln, reads=[f"ss{i}", "epsD"], writes=[f"ss{i}"], scale=1.0 / D, bias=epsD[:, 0:1])
                            P.act(ss[:, i:i + 1], ss[:, i:i + 1], AF.Exp, reads=[f"ss{i}"], writes=[f"ss{i}"], scale=-0.5)
                            P.stt("dve", xres[:, i, :], xres[:, i, :], ss[:, i:i + 1], fnb[:], ALU.mult, ALU.mult,
                                  reads=[f"xres{i}", f"ss{i}", "fnb"], writes=[f"xres{i}"])
                            P.dma("sp", ov[i], xres[:, i, :], reads=[f"xres{i}"])
                emit_phase()
    return nc


def inputs_fused(inp):
    c = make_consts()
    w2 = fft_consts_packed(c)
    x = inp["x"]
    g0 = np.ascontiguousarray(inp["norm_even"][0].reshape(8, 128).T)
    g1 = np.ascontiguousarray(inp["norm_odd"][0].reshape(8, 128).T)
    cw = np.ascontiguousarray(inp["conv_w"][0].reshape(3, 4, 128).transpose(2, 1, 0).reshape(128, 12))
    gq, gk = inp["q_gain"][0], inp["k_gain"][0]
    partner = np.arange(128) ^ 1
    gv = np.ascontiguousarray(np.stack([gq, gq[partner], gk, gk[partner]], axis=1).astype(np.float32))
    fnb = np.ascontiguousarray(np.broadcast_to(inp["final_norm"][None, :], (128, D)).astype(np.float32))
    gb = np.ascontiguousarray(np.broadcast_to(np.stack([gq, gk])[None], (128, 2, 128)).astype(np.float32))
    shared = dict(g0=g0, w_in0=np.ascontiguousarray(inp["w_in_even"][0]), cw=cw, ident=c["ident"], c1s1=c["c1s1"], w2=w2,
                  ccsc=c["ccsc"], w_out0=np.ascontiguousarray(inp["w_out_even"][0]), g1=g1,
                  w_in1=np.ascontiguousarray(inp["w_in_odd"][0]), gv=gv, p0=c["p0"], ones=c["ones"],
                  w_out1=np.ascontiguousarray(inp["w_out_odd"][0]), fnb=fnb, gb=gb)
    maps = []
    for core in range(NCORES):
        b, q = core // 4, core % 4
        cosT, sinT = rope_tables(core)
        m = dict(shared)
        m.update(x=np.ascontiguousarray(x[b, q * T:(q + 1) * T]), xh=halo_rows(x, core), cos=cosT, sin=sinT)
        maps.append(m)
    return maps


def kernel_fused(inp):
    if "F" not in _NC_CACHE:
        _NC_CACHE["F"] = build_fused()
    res = run_bass_kernel_spmd(_NC_CACHE["F"], inputs_fused(inp), core_ids=list(range(NCORES))).results
    out = np.stack([np.concatenate([res[b * 4 + q]["out"] for q in range(4)], 0) for b in range(2)])
    return out.astype(np.float32)


def kernel(x, norm_even, w_in_even, conv_w, w_out_even, norm_odd, w_in_odd, q_gain, k_gain, w_out_odd, final_norm):
    inp = dict(x=np.asarray(x, np.float32), norm_even=np.asarray(norm_even, np.float32),
               w_in_even=np.asarray(w_in_even, np.float32), conv_w=np.asarray(conv_w, np.float32),
               w_out_even=np.asarray(w_out_even, np.float32), norm_odd=np.asarray(norm_odd, np.float32),
               w_in_odd=np.asarray(w_in_odd, np.float32), q_gain=np.asarray(q_gain, np.float32),
               k_gain=np.asarray(k_gain, np.float32), w_out_odd=np.asarray(w_out_odd, np.float32),
               final_norm=np.asarray(final_norm, np.float32))
    return kernel_fused(inp)
```
